# Optimizing a Trainium2 kernel written in Bass

```python
import math, functools
import jax, jax.numpy as jnp
from jax import lax
import numpy as np

D_MODEL = 1024
BATCH = 32
SEQ = 2048
DEPTH = 4
DEC_BATCH = 1
DEC_SEQ = 16384
PAST_LEN = 128

N_EVEN = (DEPTH + 1) // 2
N_ODD = DEPTH // 2

A_HEADS = 4
A_HEAD_DIM = 128
A_WIDTH = A_HEADS * A_HEAD_DIM
A_CHUNK = 64
B_HEADS = 8
B_KV_HEADS = 2
B_GROUP = B_HEADS // B_KV_HEADS
B_HEAD_DIM = 64
B_WIDTH = B_HEADS * B_HEAD_DIM
B_KV_WIDTH = B_KV_HEADS * B_HEAD_DIM
WINDOW = 128
ATTN_BLOCK = 128
AB_IN = 5 * A_WIDTH + B_WIDTH + 2 * B_KV_WIDTH
AB_MIX = A_WIDTH + B_WIDTH
C_WIDTH = 2 * D_MODEL
C_GROUPS = 8
C_GROUP_DIM = C_WIDTH // C_GROUPS
C_CHUNK = 128
FFN_DIM = ((8 * D_MODEL + 3 * 256 - 1) // (3 * 256)) * 256
DEEPNORM_ALPHA = (2.0 * DEPTH) ** 0.25
DEEPNORM_BETA = (8.0 * DEPTH) ** -0.25
LN_EPS = 1e-5
RMS_EPS = 1e-6

kernel_name = 'hgrn2_swa_gmlp_deepnorm_encoder'


def _layernorm(x, g, b):
    xf = x.astype(jnp.float32)
    mu = jnp.mean(xf, axis=-1, keepdims=True)
    var = jnp.mean(jnp.square(xf - mu), axis=-1, keepdims=True)
    return ((xf - mu) * lax.rsqrt(var + LN_EPS) * g.astype(jnp.float32) + b.astype(jnp.float32)).astype(x.dtype)


def _split(z, sizes):
    out, off = [], 0
    for s in sizes:
        out.append(z[..., off:off + s])
        off += s
    return out


def _gla_scan(q, k, v, logf):
    B, L, H, dk = q.shape
    dv = v.shape[-1]
    nc = L // A_CHUNK

    def chunks(a):
        return a.reshape(B, nc, A_CHUNK, H, a.shape[-1]).transpose(1, 0, 3, 2, 4)

    lower = jnp.tril(jnp.ones((A_CHUNK, A_CHUNK), dtype=bool))[:, :, None]

    def step(S, inp):
        qc, kc, vc, gc = inp
        b = jnp.cumsum(gc, axis=2)
        b_last = b[:, :, -1:, :]
        o = jnp.einsum('bhtd,bhde->bhte', qc * jnp.exp(b), S)
        diff = b[:, :, :, None, :] - b[:, :, None, :, :]
        decay = jnp.exp(jnp.where(lower, diff, -jnp.inf))
        scores = jnp.einsum('bhtd,bhtsd->bhts', qc, decay * kc[:, :, None, :, :])
        o = o + jnp.einsum('bhts,bhse->bhte', scores, vc)
        S = jnp.exp(b_last[:, :, 0, :, None]) * S + jnp.einsum('bhsd,bhse->bhde', kc * jnp.exp(b_last - b), vc)
        return S, o

    S0 = jnp.zeros((B, H, dk, dv), jnp.float32)
    _, o = lax.scan(step, S0, (chunks(q), chunks(k), chunks(v), chunks(logf)))
    return o.transpose(1, 0, 3, 2, 4).reshape(B, L, H, dv)


def _hgrn2(q_raw, ff_raw, fb_raw, i_raw, g_raw, lb_f, lb_b, norm_g):
    B, L, _ = q_raw.shape

    def heads(a):
        return a.reshape(B, L, A_HEADS, A_HEAD_DIM).astype(jnp.float32)

    q = jax.nn.silu(heads(q_raw))
    v = heads(i_raw)

    def gates(f_raw, lb):
        f_raw = heads(f_raw)
        lb = lb.astype(jnp.float32).reshape(A_HEADS, A_HEAD_DIM)
        logf = jnp.logaddexp(jnp.log(lb), jnp.log1p(-lb) + jax.nn.log_sigmoid(f_raw))
        k = (1.0 - lb) * jax.nn.sigmoid(-f_raw)
        return k, logf

    k_f, g_f = gates(ff_raw, lb_f)
    k_b, g_b = gates(fb_raw, lb_b)
    flip = lambda a: jnp.flip(a, axis=1)
    o = _gla_scan(q, k_f, v, g_f) + flip(_gla_scan(flip(q), flip(k_b), flip(v), flip(g_b)))
    o = o * lax.rsqrt(jnp.mean(jnp.square(o), axis=-1, keepdims=True) + RMS_EPS) * norm_g.astype(jnp.float32)
    o = o * jax.nn.silu(heads(g_raw))
    return o.reshape(B, L, A_WIDTH)


def _window_gqa(q_raw, k_raw, v_raw, sink):
    B, L, _ = q_raw.shape
    nb = L // ATTN_BLOCK
    q = q_raw.reshape(B, nb, ATTN_BLOCK, B_KV_HEADS, B_GROUP, B_HEAD_DIM).astype(jnp.float32)

    def neighbourhood(a):
        a = a.reshape(B, L, B_KV_HEADS, B_HEAD_DIM).astype(jnp.float32)
        ap = jnp.pad(a, ((0, 0), (ATTN_BLOCK, ATTN_BLOCK), (0, 0), (0, 0)))
        return jnp.concatenate(
            [ap[:, j * ATTN_BLOCK:j * ATTN_BLOCK + L].reshape(B, nb, ATTN_BLOCK, B_KV_HEADS, B_HEAD_DIM)
             for j in range(3)], axis=2)

    kb, vb = neighbourhood(k_raw), neighbourhood(v_raw)
    t = jnp.arange(ATTN_BLOCK)[:, None]
    s = jnp.arange(3 * ATTN_BLOCK)[None, :]
    dist = jnp.abs(t - s + ATTN_BLOCK)
    kpos = jnp.arange(nb)[:, None] * ATTN_BLOCK + jnp.arange(3 * ATTN_BLOCK)[None, :] - ATTN_BLOCK
    valid = (dist <= WINDOW)[None] & ((kpos >= 0) & (kpos < L))[:, None, :]
    slopes = jnp.exp2(-(8.0 / B_HEADS) * jnp.arange(1, B_HEADS + 1, dtype=jnp.float32))
    slopes = slopes.reshape(B_KV_HEADS, B_GROUP)[:, :, None, None]
    scores = jnp.einsum('bnqhgd,bnkhd->bnhgqk', q, kb) * (B_HEAD_DIM ** -0.5)
    scores = scores - slopes * dist.astype(jnp.float32)
    scores = jnp.where(valid[None, :, None, None], scores, -jnp.inf)
    sink_l = sink.astype(jnp.float32).reshape(B_KV_HEADS, B_GROUP)[:, :, None, None]
    m = jnp.maximum(jnp.max(scores, axis=-1, keepdims=True), sink_l)
    p = jnp.exp(scores - m)
    denom = jnp.sum(p, axis=-1, keepdims=True) + jnp.exp(sink_l - m)
    o = jnp.einsum('bnhgqk,bnkhd->bnqhgd', p / denom, vb)
    return o.reshape(B, L, B_WIDTH)


def _sgu(x, w_in, ln_g, ln_b, w_s, b_s, w_out):
    B, L, _ = x.shape
    z = jax.nn.gelu(x @ w_in, approximate=False)
    u, v = jnp.split(z, 2, axis=-1)
    v = _layernorm(v, ln_g, ln_b)
    v = v.reshape(B, L // C_CHUNK, C_CHUNK, C_GROUPS, C_GROUP_DIM)
    s = jnp.einsum('gts,bnsgc->bntgc', w_s, v) + b_s.T[:, :, None]
    return (u * s.reshape(B, L, C_WIDTH)) @ w_out


def _swiglu(x, wg, wu, wd):
    return (jax.nn.silu(x @ wg) * (x @ wu)) @ wd


def _trunk(x, w_in_ab, hgrn_lb_logits, hgrn_norm_g, attn_sink, w_out_ab, w_in_c, c_ln_g, c_ln_b, c_ws, c_bs,
           w_out_c, ffn_w_gate, ffn_w_up, ffn_w_down, ln_mix_g, ln_mix_b, ln_ffn_g, ln_ffn_b):
    p = jax.nn.softmax(hgrn_lb_logits.astype(jnp.float32), axis=0)
    lower_bounds = jnp.maximum(jnp.cumsum(p, axis=0) - p[0:1], 0.0)
    for layer in range(DEPTH):
        j = layer // 2
        if layer % 2 == 0:
            z = x @ w_in_ab[j]
            q_a, ff_a, fb_a, i_a, g_a, q_b, k_b, v_b = _split(
                z, (A_WIDTH,) * 5 + (B_WIDTH, B_KV_WIDTH, B_KV_WIDTH))
            o_a = _hgrn2(q_a, ff_a, fb_a, i_a, g_a, lower_bounds[j, 0], lower_bounds[j, 1], hgrn_norm_g[j])
            o_b = _window_gqa(q_b, k_b, v_b, attn_sink[j])
            y = jnp.concatenate([o_a.astype(x.dtype), o_b.astype(x.dtype)], axis=-1) @ w_out_ab[j]
        else:
            y = _sgu(x, w_in_c[j], c_ln_g[j], c_ln_b[j], c_ws[j], c_bs[j], w_out_c[j])
        x = _layernorm(DEEPNORM_ALPHA * x + y, ln_mix_g[layer], ln_mix_b[layer])
        x = _layernorm(DEEPNORM_ALPHA * x + _swiglu(x, ffn_w_gate[layer], ffn_w_up[layer], ffn_w_down[layer]),
                       ln_ffn_g[layer], ln_ffn_b[layer])
    return x


def setup_inputs(seed: int = 0) -> dict:
    key = jax.random.key(seed)
    ks = jax.random.split(key, 20)

    def nrm(k, shape, scale):
        return jax.random.normal(k, shape, jnp.float32) * scale

    return {
        'x_prompt': nrm(ks[0], (BATCH, SEQ, D_MODEL), 1.0),
        'x_sample': nrm(ks[1], (DEC_BATCH, DEC_SEQ, D_MODEL), 1.0),
        'w_in_ab': nrm(ks[2], (N_EVEN, D_MODEL, AB_IN), D_MODEL ** -0.5),
        'hgrn_lb_logits': nrm(ks[3], (N_EVEN, 2, A_WIDTH), 0.5),
        'hgrn_norm_g': 1.0 + nrm(ks[4], (N_EVEN, A_HEAD_DIM), 0.02),
        'attn_sink': nrm(ks[5], (N_EVEN, B_HEADS), 0.5),
        'w_out_ab': nrm(ks[6], (N_EVEN, AB_MIX, D_MODEL), AB_MIX ** -0.5 * DEEPNORM_BETA),
        'w_in_c': nrm(ks[7], (N_ODD, D_MODEL, 2 * C_WIDTH), D_MODEL ** -0.5),
        'c_ln_g': 1.0 + nrm(ks[8], (N_ODD, C_WIDTH), 0.02),
        'c_ln_b': nrm(ks[9], (N_ODD, C_WIDTH), 0.02),
        'c_ws': nrm(ks[10], (N_ODD, C_GROUPS, C_CHUNK, C_CHUNK), C_CHUNK ** -0.5),
        'c_bs': 1.0 + nrm(ks[11], (N_ODD, C_GROUPS, C_CHUNK), 0.02),
        'w_out_c': nrm(ks[12], (N_ODD, C_WIDTH, D_MODEL), C_WIDTH ** -0.5 * DEEPNORM_BETA),
        'ffn_w_gate': nrm(ks[13], (DEPTH, D_MODEL, FFN_DIM), D_MODEL ** -0.5),
        'ffn_w_up': nrm(ks[14], (DEPTH, D_MODEL, FFN_DIM), D_MODEL ** -0.5 * DEEPNORM_BETA),
        'ffn_w_down': nrm(ks[15], (DEPTH, FFN_DIM, D_MODEL), FFN_DIM ** -0.5 * DEEPNORM_BETA),
        'ln_mix_g': 1.0 + nrm(ks[16], (DEPTH, D_MODEL), 0.02),
        'ln_mix_b': nrm(ks[17], (DEPTH, D_MODEL), 0.02),
        'ln_ffn_g': 1.0 + nrm(ks[18], (DEPTH, D_MODEL), 0.02),
        'ln_ffn_b': nrm(ks[19], (DEPTH, D_MODEL), 0.02),
    }


def reference(x_prompt, x_sample, w_in_ab, hgrn_lb_logits, hgrn_norm_g, attn_sink, w_out_ab, w_in_c, c_ln_g,
              c_ln_b, c_ws, c_bs, w_out_c, ffn_w_gate, ffn_w_up, ffn_w_down, ln_mix_g, ln_mix_b, ln_ffn_g,
              ln_ffn_b):
    trunk = functools.partial(
        _trunk, w_in_ab=w_in_ab, hgrn_lb_logits=hgrn_lb_logits, hgrn_norm_g=hgrn_norm_g, attn_sink=attn_sink,
        w_out_ab=w_out_ab, w_in_c=w_in_c, c_ln_g=c_ln_g, c_ln_b=c_ln_b, c_ws=c_ws, c_bs=c_bs, w_out_c=w_out_c,
        ffn_w_gate=ffn_w_gate, ffn_w_up=ffn_w_up, ffn_w_down=ffn_w_down, ln_mix_g=ln_mix_g, ln_mix_b=ln_mix_b,
        ln_ffn_g=ln_ffn_g, ln_ffn_b=ln_ffn_b)
    y_prompt = trunk(x_prompt)
    y_sample = trunk(x_sample)
    return (y_prompt, y_sample)
```

```python
import numpy as np
import ml_dtypes
import concourse.bass as bass
import concourse.mybir as mybir
from concourse.bass_utils import run_bass_kernel_spmd

F32 = mybir.dt.float32
BF16 = mybir.dt.bfloat16
AF = mybir.ActivationFunctionType
ALU = mybir.AluOpType

D = 1024
KC = 8
T = 512
UNIT = 2048
TPU = UNIT // T
FFN = 2816
FC = 22
DEPTH = 4
ALPHA = (2.0 * DEPTH) ** 0.25
LN_EPS = 1e-5
RMS_EPS = 1e-6
NEG = -30000.0
MID = 31


class _Op:
    __slots__ = ("eng", "fn", "deps", "is_dma", "dsem", "dval", "sig", "sval", "ssem")

    def __init__(self, eng, fn, is_dma=False, dsem=None):
        self.eng = eng
        self.fn = fn
        self.deps = set()
        self.is_dma = is_dma
        self.dsem = dsem
        self.dval = 0
        self.sig = False
        self.sval = 0
        self.ssem = 0


class Prog:
    ENGS = ("pe", "act", "dve", "pool", "sp")
    SEM_ROLL = 20000

    def __init__(self, nc):
        self.nc = nc
        self.ops = []
        self.last_w = {}
        self.readers = {}
        self.dsem_tot = {}

    def _track(self, op, idx, reads, writes):
        deps = op.deps
        reads, writes = list(reads), list(writes)
        if any(k[0].startswith("a_") for k in reads + writes):
            reads.append(("arena",))
        for k in reads:
            w = self.last_w.get(k)
            if w is not None:
                deps.add(w)
        for k in writes:
            w = self.last_w.get(k)
            if w is not None:
                deps.add(w)
            for r in self.readers.get(k, ()):
                deps.add(r)
        deps.discard(idx)
        for k in reads:
            self.readers.setdefault(k, []).append(idx)
        for k in writes:
            self.last_w[k] = idx
            self.readers[k] = []

    def op(self, eng, fn, reads=(), writes=()):
        o = _Op(eng, fn)
        idx = len(self.ops)
        self.ops.append(o)
        self._track(o, idx, reads, writes)
        return idx

    def dma(self, eng, out, in_, reads=(), writes=(), dsem="d"):
        o = _Op(eng, (out, in_), is_dma=True, dsem=dsem)
        self.dsem_tot[dsem] = self.dsem_tot.get(dsem, 0) + 16
        o.dval = self.dsem_tot[dsem]
        idx = len(self.ops)
        self.ops.append(o)
        self._track(o, idx, reads, writes)
        return idx

    def emit(self, stack):
        nc = self.nc
        ops = self.ops
        for o in ops:
            for d in o.deps:
                if not ops[d].is_dma and not (o.eng == "pe" and ops[d].eng == "pe" and not o.is_dma):
                    ops[d].sig = True
        cnt = {e: 0 for e in self.ENGS}
        nsem = {e: 1 for e in self.ENGS}
        for o in ops:
            if o.sig:
                cnt[o.eng] += 1
                if cnt[o.eng] > self.SEM_ROLL:
                    cnt[o.eng] = 1
                    nsem[o.eng] += 1
                o.sval = cnt[o.eng]
                o.ssem = nsem[o.eng] - 1
        sems = {e: [stack.enter_context(nc.semaphore(f"s_{e}_{i}")) for i in range(nsem[e])] for e in self.ENGS}
        dsems = {k: stack.enter_context(nc.semaphore(f"d_{k}")) for k in self.dsem_tot}
        fin = stack.enter_context(nc.semaphore("fin"))
        block = stack.enter_context(nc.Block())
        out_dmas = [o for o in ops if o.is_dma and o.dsem.startswith("out")]
        last_out = {}
        for o in out_dmas:
            last_out[o.dsem] = o.dval

        def stream(ename):
            def body(e):
                waited = {}
                for o in ops:
                    if o.eng != ename:
                        continue
                    need = {}
                    for d in o.deps:
                        dd = ops[d]
                        if dd.is_dma:
                            key = ("d", dd.dsem)
                            val = dd.dval
                        else:
                            if dd.eng == ename and ename == "pe":
                                continue
                            key = (dd.eng, dd.ssem)
                            val = dd.sval
                        if val > need.get(key, 0):
                            need[key] = val
                    for key, val in need.items():
                        if waited.get(key, 0) >= val:
                            continue
                        waited[key] = val
                        if key[0] == "d":
                            e.wait_ge(dsems[key[1]], val)
                        else:
                            e.wait_ge(sems[key[0]][key[1]], val)
                    if o.is_dma:
                        e.dma_start(out=o.fn[0], in_=o.fn[1]).then_inc(dsems[o.dsem], 16)
                    else:
                        ins = o.fn(e)
                        if o.sig:
                            ins.then_inc(sems[ename][o.ssem], 1)
                if ename == "act":
                    for k, v in last_out.items():
                        e.wait_ge(dsems[k], v)
            return body

        block.tensor(stream("pe"))
        block.scalar(stream("act"))
        block.vector(stream("dve"))
        block.gpsimd(stream("pool"))
        block.sync(stream("sp"))


def _fm(w):
    K, N = w.shape
    return np.ascontiguousarray(w.reshape(K // 128, 128, N).transpose(1, 0, 2).reshape(128, -1))


def build_slabs(inp):
    slabs, index = [], {}
    off = 0

    def add(name, arr):
        nonlocal off
        arr = np.ascontiguousarray(arr, dtype=np.float32)
        assert arr.shape[0] == 128 and arr.shape[1] <= 4096, (name, arr.shape)
        slabs.append(arr)
        index[name] = (off, arr.shape[1])
        off += arr.shape[1]

    for j in range(2):
        w = inp["w_in_ab"][j]
        add(f"e{j}_ff", _fm(w[:, 512:1024]))
        add(f"e{j}_fb", _fm(w[:, 1024:1536]))
        add(f"e{j}_qa", _fm(w[:, 0:512]))
        add(f"e{j}_ga", _fm(w[:, 2048:2560]))
        add(f"e{j}_qb", _fm(w[:, 2560:3072]))
        kd = np.concatenate([w[:, 3072:3136], w[:, 3072:3136], w[:, 3136:3200], w[:, 3136:3200]], axis=1)
        add(f"e{j}_kd", _fm(kd))
        add(f"e{j}_ia", _fm(w[:, 1536:2048]))
        add(f"e{j}_vb", _fm(w[:, 3200:3328]))
        wo = inp["w_out_ab"][j]
        add(f"e{j}_wo0", _fm(wo[:, 0:512]))
        add(f"e{j}_wo1", _fm(wo[:, 512:1024]))
        wc = inp["w_in_c"][j]
        for s in range(4):
            add(f"o{j}_v{s}", _fm(wc[:, 2048 + 512 * s:2048 + 512 * (s + 1)]))
        for s in range(4):
            add(f"o{j}_u{s}", _fm(wc[:, 512 * s:512 * (s + 1)]))
        woc = inp["w_out_c"][j]
        for s in range(4):
            add(f"o{j}_wo{s}", _fm(woc[:, 256 * s:256 * (s + 1)]))
        add(f"o{j}_ws", inp["c_ws"][j].transpose(2, 0, 1).reshape(128, 1024))
    for l in range(4):
        g, u, d = inp["ffn_w_gate"][l], inp["ffn_w_up"][l], inp["ffn_w_down"][l]
        for s in range(6):
            add(f"f{l}_g{s}", _fm(g[:, 512 * s:min(512 * (s + 1), FFN)]))
            add(f"f{l}_u{s}", _fm(u[:, 512 * s:min(512 * (s + 1), FFN)]))
        for m in range(8):
            add(f"f{l}_d{m}", _fm(d[:, 128 * m:128 * (m + 1)]))
    return slabs, index


def build_consts(inp, flags):
    cols, index = [], {}
    off = 0

    def add(name, arr):
        nonlocal off
        arr = np.ascontiguousarray(arr, dtype=np.float32).reshape(128, -1)
        cols.append(arr)
        index[name] = (off, arr.shape[1])
        off += arr.shape[1]

    add("ident", np.eye(128, dtype=np.float32))
    s = np.arange(128)[:, None]
    t = np.arange(128)[None, :]
    same = (s // 64) == (t // 64)
    add("trif", np.tile((same & (s <= t)).astype(np.float32), (1, 4)))
    add("trib", np.tile((same & (s >= t)).astype(np.float32), (1, 4)))
    sm = np.ones((128, 512), np.float32)
    sm[:, ::64] = 0.0
    add("scanmask", sm)
    k = np.arange(128)[:, None]
    q = np.arange(384)[None, :]
    dist = np.abs(q - 128 - k).astype(np.float32)
    add("mdist", -8.0 * np.where(dist <= 128, dist, 1.0e6))
    nu = flags.shape[0]
    add("flags", np.tile(flags[None, :], (128, 1)))
    add("nbias", np.tile(((flags - 1.0) * 30000.0)[None, :], (128, 1)))
    lnp = np.stack([inp["ln_mix_g"], inp["ln_mix_b"], inp["ln_ffn_g"], inp["ln_ffn_b"]], 0)
    add("lnp", lnp.reshape(4, 4, 8, 128).transpose(3, 0, 1, 2))
    add("ng", inp["hgrn_norm_g"].T)
    add("lbl", inp["hgrn_lb_logits"].reshape(2, 2, 4, 128).transpose(3, 0, 1, 2))
    sk = inp["attn_sink"].reshape(2, 4, 2)
    add("sink", np.repeat(sk.transpose(2, 0, 1)[:, None], 64, axis=1).reshape(128, 8))
    add("clng", inp["c_ln_g"].reshape(2, 16, 128).transpose(2, 0, 1))
    add("clnb", inp["c_ln_b"].reshape(2, 16, 128).transpose(2, 0, 1))
    cbs = np.ascontiguousarray(np.tile(inp["c_bs"].reshape(1, 2 * 1024), (128, 1)), dtype=np.float32)
    return np.concatenate(cols, axis=1), index, cbs


class Mem:
    def __init__(self, arena, nbytes):
        self.arena = arena
        self.nbytes = nbytes
        self.top = 0

    def mark(self):
        return self.top

    def reset(self, m):
        self.top = m

    def alloc(self, free_shape, dtype, at=None):
        esz = 4 if dtype == F32 else 2
        n = int(np.prod(free_shape))
        nb = (n * esz + 31) // 32 * 32
        off = self.top if at is None else at
        assert off + nb <= self.nbytes, ("SBUF map overflow", off + nb, self.nbytes)
        if at is None:
            self.top = off + nb
        v = self.arena[:, off // 4:(off + nb) // 4]
        if dtype != F32:
            v = v.bitcast(dtype)
        v = v[:, 0:n]
        if len(free_shape) == 2:
            v = v.rearrange("p (a b) -> p a b", b=free_shape[1])
        elif len(free_shape) == 3:
            v = v.rearrange("p (a b c) -> p a b c", b=free_shape[1], c=free_shape[2])
        elif len(free_shape) == 4:
            v = v.rearrange("p (a b c d) -> p a b c d", b=free_shape[1], c=free_shape[2], d=free_shape[3])
        return v


class Builder:
    def __init__(self, nc, P, NU, windex, cindex):
        self.nc, self.P, self.NU = nc, P, NU
        self.NT = NU * TPU
        self.windex, self.cindex = windex, cindex
        self.psn = 0
        self.wsn = 0
        self.fence_n = 0

    def mm(self, out, lhsT, rhs, start, stop, r, w):
        return self.P.op("pe", lambda e: e.matmul(out, lhsT=lhsT, rhs=rhs, start=start, stop=stop,
                                                  skip_group_check=True), r, w)

    def tr(self, out, in_, ident, r, w):
        return self.P.op("pe", lambda e: e.transpose(out, in_, ident), r, w)

    def act(self, out, in_, func, r, w, bias=None, scale=None):
        kw = {}
        if bias is not None:
            kw["bias"] = bias
        if scale is not None:
            kw["scale"] = scale
        return self.P.op("act", lambda e: e.activation(out=out, in_=in_, func=func, **kw), r, w)

    def tt(self, eng, out, in0, in1, op, r, w):
        return self.P.op(eng, lambda e: e.tensor_tensor(out=out, in0=in0, in1=in1, op=op), r, w)

    def ts(self, eng, out, in0, s1, s2, op0, op1, r, w):
        if s2 is None:
            return self.P.op(eng, lambda e: e.tensor_scalar(out=out, in0=in0, scalar1=s1, scalar2=None, op0=op0), r, w)
        return self.P.op(eng, lambda e: e.tensor_scalar(out=out, in0=in0, scalar1=s1, scalar2=s2, op0=op0, op1=op1), r, w)

    def stt(self, eng, out, in0, scalar, in1, op0, op1, r, w):
        return self.P.op(eng, lambda e: e.scalar_tensor_tensor(out=out, in0=in0, scalar=scalar, in1=in1,
                                                               op0=op0, op1=op1), r, w)

    def cp(self, eng, out, in_, r, w):
        if eng == "act":
            return self.P.op("act", lambda e: e.activation(out=out, in_=in_, func=AF.Copy), r, w)
        return self.P.op(eng, lambda e: e.tensor_copy(out=out, in_=in_), r, w)

    def memset(self, eng, ap, val, w):
        return self.P.op(eng, lambda e: e.memset(ap, val), (), w)

    def bank(self):
        b = self.psn % 4
        self.psn += 1
        return b

    def fence(self):
        self.fence_n += 1
        t = self.fz
        self.P.op("dve", lambda e: e.memset(t, 0.0), (), [("arena",)])

    def wload(self, name):
        off, W = self.windex[name]
        slot = self.wsn % self.NSLOT
        self.wsn += 1
        dst = self.wslot[:, slot, 0:W]
        self.P.dma("sp", dst, self.wbf[name], [("wbf",)], [("w", slot)], dsem=f"w{slot}")
        return slot, self.wslot[:, slot, :]

    def ln_accum(self, m, ypsum, ybank):
        xT, rb, rsq, ps = self.xT, self.rb, self.rsq, self.ps
        self.stt("dve", xT[:, m, :], xT[:, m, :], ALPHA, ypsum, ALU.mult, ALU.add,
                 [("xT", m), ("ps", ybank)], [("xT", m)])
        self.cp("pool", rb[:, m % 2, :], xT[:, m, :], [("xT", m)], [("rb", m % 2)])
        self.act(rsq[:, m % 2, :], xT[:, m, :], AF.Square, [("xT", m)], [("rsq", m % 2)])
        self.mm(ps[:, 4, :], self.onesb, rb[:, m % 2, :], m == 0, m == 7, [("rb", m % 2), ("c",)], [("ps", 4)])
        self.mm(ps[:, 5, :], self.onesb, rsq[:, m % 2, :], m == 0, m == 7, [("rsq", m % 2), ("c",)], [("ps", 5)])

    def ln_finish(self, g, b, want_bf=True):
        xT, xb, st, ps = self.xT, self.xb, self.stat, self.ps
        self.ts("dve", st[:, 0, :], ps[:, 4, :], 1.0 / D, None, ALU.mult, None, [("ps", 4)], [("st", 0)])
        self.tt("dve", st[:, 1, :], st[:, 0, :], st[:, 0, :], ALU.mult, [("st", 0)], [("st", 1)])
        self.stt("dve", st[:, 2, :], ps[:, 5, :], 1.0 / D, st[:, 1, :], ALU.mult, ALU.subtract,
                 [("ps", 5), ("st", 1)], [("st", 2)])
        self.act(st[:, 3, :], st[:, 2, :], AF.Ln, [("st", 2)], [("st", 3)], bias=LN_EPS)
        self.act(st[:, 3, :], st[:, 3, :], AF.Exp, [("st", 3)], [("st", 3)], scale=-0.5)
        for m in range(KC):
            self.tt("dve", xT[:, m, :], xT[:, m, :], st[:, 0, :], ALU.subtract, [("xT", m), ("st", 0)], [("xT", m)])
        for m in range(KC):
            self.tt("pool", xT[:, m, :], xT[:, m, :], st[:, 3, :], ALU.mult, [("xT", m), ("st", 3)], [("xT", m)])
        for m in range(KC):
            self.act(xT[:, m, :], xT[:, m, :], AF.Identity, [("xT", m), ("c",)], [("xT", m)],
                     bias=b[:, m:m + 1], scale=g[:, m:m + 1])
        if want_bf:
            for m in range(KC):
                self.cp("pool", xb[:, m, :], xT[:, m, :], [("xT", m)], [("xb", m)])

    def lnp(self, kind, layer):
        o, _ = self.cindex["lnp"]
        base = o + (kind * 4 + layer) * 8
        return self.cst[:, base:base + 8]

    def ffn(self, l):
        ps, h, sg, xb = self.ps, self.h, self.sg, self.xb
        for s in range(6):
            ncs = 4 if s < 5 else 2
            gs_, gv = self.wload(f"f{l}_g{s}")
            us_, uv = self.wload(f"f{l}_u{s}")
            gv = gv[:, 0:8 * ncs * 128].rearrange("p (a b) -> p a b", b=ncs * 128)
            uv = uv[:, 0:8 * ncs * 128].rearrange("p (a b) -> p a b", b=ncs * 128)
            for c in range(ncs):
                j = 4 * s + c
                ba, bb = self.bank(), self.bank()
                for k in range(KC):
                    self.mm(ps[:, ba, :], gv[:, k, c * 128:(c + 1) * 128], xb[:, k, :], k == 0, k == 7,
                            [("w", gs_), ("xb", k)], [("ps", ba)])
                for k in range(KC):
                    self.mm(ps[:, bb, :], uv[:, k, c * 128:(c + 1) * 128], xb[:, k, :], k == 0, k == 7,
                            [("w", us_), ("xb", k)], [("ps", bb)])
                self.act(sg[:, j % 2, :], ps[:, ba, :], AF.Silu, [("ps", ba)], [("a_sg", j % 2)])
                self.tt("dve", h[:, j, :], sg[:, j % 2, :], ps[:, bb, :], ALU.mult,
                        [("a_sg", j % 2), ("ps", bb)], [("a_h", j)])
        for m in range(KC):
            ds_, dv = self.wload(f"f{l}_d{m}")
            dv = dv[:, 0:FC * 128].rearrange("p (a b) -> p a b", b=128)
            bo = self.bank()
            for k in range(FC):
                self.mm(ps[:, bo, :], dv[:, k, :], h[:, k, :], k == 0, k == FC - 1,
                        [("w", ds_), ("a_h", k)], [("ps", bo)])
            self.ln_accum(m, ps[:, bo, :], bo)
        self.ln_finish(self.lnp(2, l), self.lnp(3, l))

    def sgu(self, j, layer):
        ps, xb = self.ps, self.xb
        vtok, vn, yin, uc, tmp, bst = self.vtok, self.vn, self.yin, self.uc, self.tmp, self.bst
        ci = self.cindex
        for s in range(4):
            ws_, wv = self.wload(f"o{j}_v{s}")
            wv = wv.rearrange("p (a b) -> p a b", b=512)
            for blk in range(4):
                ba = self.bank()
                for k in range(KC):
                    self.mm(ps[:, ba, :], xb[:, k, blk * 128:(blk + 1) * 128], wv[:, k, :], k == 0, k == 7,
                            [("w", ws_), ("xb", k)], [("ps", ba)])
                self.act(vtok[:, blk, s * 512:(s + 1) * 512], ps[:, ba, :], AF.Gelu, [("ps", ba)], [("a_vtok", blk, s)])
        import os
        dbg = os.environ.get("KDBG", "")
        if "s1" in dbg:
            return
        for blk in range(4):
            for s in range(4):
                self.P.op("dve", (lambda o, i: (lambda e: e.bn_stats(out=o, in_=i)))(bst[:, blk, s, :], vtok[:, blk, s * 512:(s + 1) * 512]),
                          [("a_vtok", blk, s)], [("a_bst", blk, s)])
            mv = self.bmv[:, blk, :]
            self.P.op("dve", (lambda o, i: (lambda e: e.bn_aggr(out=o, in_=i)))(mv, bst[:, blk, :, :]),
                      [("a_bst", blk, s) for s in range(4)], [("a_bmv", blk)])
            self.act(mv[:, 1:2], mv[:, 1:2], AF.Ln, [("a_bmv", blk)], [("a_bmv", blk)], bias=LN_EPS)
            self.act(mv[:, 1:2], mv[:, 1:2], AF.Exp, [("a_bmv", blk)], [("a_bmv", blk)], scale=-0.5)
            self.stt("dve", mv[:, 0:1], mv[:, 0:1], -1.0, mv[:, 1:2], ALU.mult, ALU.mult, [("a_bmv", blk)], [("a_bmv", blk)])
            self.ts("dve", vn[:, blk, :], vtok[:, blk, :], mv[:, 1:2], mv[:, 0:1], ALU.mult, ALU.add,
                    [("a_bmv", blk)] + [("a_vtok", blk, s) for s in range(4)], [("a_vn", blk)])
        og, _ = ci["clng"]
        if "s2" in dbg:
            return
        for s in range(4):
            ws_, wu = self.wload(f"o{j}_u{s}")
            wu = wu.rearrange("p (a b) -> p a b", b=512)
            for c in range(4):
                fc = 4 * s + c
                g = fc // 2
                ba = self.bank()
                for k in range(KC):
                    self.mm(ps[:, ba, :], wu[:, k, c * 128:(c + 1) * 128], xb[:, k, :], k == 0, k == 7,
                            [("w", ws_), ("xb", k)], [("ps", ba)])
                self.act(uc[:, fc % 2, :], ps[:, ba, :], AF.Gelu, [("ps", ba)], [("a_uc", fc % 2)])
                b2 = 4 + fc % 4
                for blk in range(4):
                    self.mm(ps[:, b2, blk * 128:(blk + 1) * 128], vn[:, blk, fc * 128:(fc + 1) * 128], self.wst[:, j, g, :],
                            True, True, [("a_vn", blk), ("c",)], [("ps", b2)])
                for blk in range(4):
                    self.stt("dve", tmp[:, fc % 2, blk * 128:(blk + 1) * 128], ps[:, b2, blk * 128:(blk + 1) * 128],
                             self.cst[:, og + j * 16 + fc:og + j * 16 + fc + 1], self.Cc[:, j, fc, :], ALU.mult, ALU.add,
                             [("ps", b2), ("c",)], [("a_tmp", fc % 2, blk)])
                self.tt("pool", yin[:, fc, :], tmp[:, fc % 2, :], uc[:, fc % 2, :], ALU.mult,
                        [("a_tmp", fc % 2, blk) for blk in range(4)] + [("a_uc", fc % 2)], [("a_yin", fc)])
        if "s3" in dbg:
            return
        for s in range(4):
            ws_, wo = self.wload(f"o{j}_wo{s}")
            wo = wo.rearrange("p (a b) -> p a b", b=256)
            for c in range(2):
                m = 2 * s + c
                bo = self.bank()
                for k in range(16):
                    self.mm(ps[:, bo, :], wo[:, k, c * 128:(c + 1) * 128], yin[:, k, :], k == 0, k == 15,
                            [("w", ws_), ("a_yin", k)], [("ps", bo)])
                self.ln_accum(m, ps[:, bo, :], bo)
        self.ln_finish(self.lnp(0, layer), self.lnp(1, layer))

    def load_x(self, layer, i):
        ps, xT, xb = self.ps, self.xT, self.xb
        if layer == 0:
            xtok = self.xtok
            src = self.xin[i * T:(i + 1) * T, :].rearrange("(b p) f -> p b f", p=128)
            self.P.dma("act", xtok, src, [], [("a_xtok",)], dsem="xtok")
            import os
            dbg = os.environ.get("KDBG", "")
            if "v3" in dbg:
                return
            pcs = self.split3(xtok, [("a_xtok",)])
            if "v4" in dbg:
                return
            for k in range(KC):
                ba = self.bank()
                for blk in range(4):
                    for n in range(3):
                        self.mm(ps[:, ba, blk * 128:(blk + 1) * 128], pcs[n][:, blk, k * 128:(k + 1) * 128], self.identb,
                                n == 0, n == 2, [("a_pc", n), ("c",)], [("ps", ba)])
                if "v2" not in dbg:
                    self.cp("dve", xT[:, k, :], ps[:, ba, :], [("ps", ba)], [("xT", k)])
                self.cp("act", xb[:, k, :], xT[:, k, :], [("xT", k)], [("xb", k)])
        else:
            self.P.dma("act", xT, self.xs[i].rearrange("p (a b) -> p a b", b=T), [("xs", i)],
                       [("xT", k) for k in range(KC)], dsem="xTl")
            for k in range(KC):
                self.cp("pool", xb[:, k, :], xT[:, k, :], [("xT", k)], [("xb", k)])

    def split3(self, src, rkeys):
        pcs = [self.pc0, self.pc1, self.pc2]
        for n in range(3):
            self.cp("dve", pcs[n], src, rkeys + [("a_spl",)], [("a_pc", n)])
            if n < 2:
                self.tt("pool", src, src, pcs[n], ALU.subtract, rkeys + [("a_pc", n), ("a_spl",)], [("a_spl",)] + rkeys)
        return pcs

    def store_x(self, layer, i):
        ps, xT = self.ps, self.xT
        if layer == 1:
            self.P.dma("act", self.xs[i].rearrange("p (a b) -> p a b", b=T), xT, [("xT", k) for k in range(KC)],
                       [("xs", i)], dsem="xTs")
        else:
            ytok = self.ytok
            xk = [("xT", k) for k in range(KC)]
            pcs = [self.pc0, self.pc1, self.pc2]
            for n in range(3):
                for k in range(KC):
                    x3 = xT[:, k, :].rearrange("p (b t) -> p b t", t=128)
                    self.cp("dve", pcs[n][:, :, k * 128:(k + 1) * 128], x3, [("xT", k)], [("a_pc", n)])
                    if n < 2:
                        self.tt("pool", x3, x3, pcs[n][:, :, k * 128:(k + 1) * 128], ALU.subtract, [("xT", k), ("a_pc", n)], [("xT", k)])
            for blk in range(4):
                for half in range(2):
                    ba = self.bank()
                    for kk in range(4):
                        k = half * 4 + kk
                        for n in range(3):
                            self.mm(ps[:, ba, kk * 128:(kk + 1) * 128], pcs[n][:, blk, k * 128:(k + 1) * 128], self.identb,
                                    n == 0, n == 2, [("a_pc", n), ("c",)], [("ps", ba)])
                    self.cp("dve" if half == 0 else "act", ytok[:, blk, half * 512:(half + 1) * 512], ps[:, ba, :],
                            [("ps", ba)], [("a_ytok", blk, half)])
            dst = self.yout[i * T:(i + 1) * T, :].rearrange("(b p) f -> p b f", p=128)
            self.P.dma("act", dst, ytok, [("a_ytok", blk, hf) for blk in range(4) for hf in range(2)], [("yout", i)],
                       dsem="outy")

    def gates(self, j, dr, hd, need_q=True):
        sgm, qs, qt, kt, cs = self.sgm, self.qs, self.qt, self.kt, self.cs
        Bt, Ct, Et = self.gB, self.gC, self.gE
        lb = self.lbt[:, j, dr, hd:hd + 1]
        oml = self.omlt[:, j, dr, hd:hd + 1]
        noml = self.nomlt[:, j, dr, hd:hd + 1]
        S = ("a_sgm", dr, hd)
        kB, kC, kE = ("a_gB",), ("a_gC",), ("a_gE",)
        kcs = ("a_cs", dr, hd)
        self.act(Bt, sgm[:, dr, hd, :], AF.Ln, [S, ("c",)], [kB], bias=lb, scale=oml)
        self.ts("dve", sgm[:, dr, hd, :], sgm[:, dr, hd, :], noml, oml, ALU.mult, ALU.add, [S, ("c",)], [S])
        self.P.op("dve", (lambda o, d0, d1: (lambda e: e.tensor_tensor_scan(out=o, data0=d0, data1=d1, initial=0.0,
                                                                              op0=ALU.mult, op1=ALU.add)))(Ct, self.scanmask, Bt),
                  [kB, ("c",)], [kC])
        mid = cs[:, dr, 3, hd, :]
        if dr == 0:
            X, kX = Ct, kC
        else:
            self.tt("dve", Bt, Bt, Ct, ALU.subtract, [kB, kC], [kB])
            X, kX = Bt, kB
        X3 = X.rearrange("p (c t) -> p c t", t=64)
        self.cp("dve", mid, X3[:, :, MID], [kX], [kcs])
        self.tt("dve", X3, X3, mid.unsqueeze(2).to_broadcast([128, 8, 64]), ALU.subtract, [kX, kcs], [kX])
        self.act(Et, X, AF.Exp, [kX], [kE])
        E3 = Et.rearrange("p (c t) -> p c t", t=64)
        w0, w1, w2 = cs[:, dr, 0, hd, :], cs[:, dr, 1, hd, :], cs[:, dr, 2, hd, :]
        if dr == 0:
            self.act(w0, mid, AF.Exp, [kcs], [kcs])
            self.cp("dve", w2, E3[:, :, 63], [kE], [kcs])
            self.tt("dve", w1, w0, w2, ALU.mult, [kcs], [kcs])
        else:
            C3 = Ct.rearrange("p (c t) -> p c t", t=64)
            self.act(w1, C3[:, :, 63], AF.Exp, [kC], [kcs])
            self.act(w0, mid, AF.Exp, [kcs], [kcs])
            self.tt("dve", w0, w0, w1, ALU.mult, [kcs], [kcs])
            self.cp("dve", w2, E3[:, :, 0], [kE], [kcs])
        if need_q:
            self.tt("dve", qt[:, dr, hd, :], qs[:, hd, :], Et, ALU.mult, [("a_qs", hd), kE], [("a_qt", dr, hd)])
        self.act(X, X, AF.Exp, [kX], [kX], scale=-1.0)
        self.tt("pool", kt[:, dr, hd, :], sgm[:, dr, hd, :], X, ALU.mult, [S, kX], [("a_kt", dr, hd)])

    def ktrans(self, dr):
        ps, kt, kT = self.ps, self.kt, self.kT
        for blk in range(4):
            ba = self.bank()
            pb = ps[:, ba, :].bitcast(BF16)
            for hd in range(4):
                self.tr(pb[:, hd * 128:(hd + 1) * 128], kt[:, dr, hd, blk * 128:(blk + 1) * 128], self.identb,
                        [("a_kt", dr, hd), ("c",)], [("ps", ba)])
            self.cp("act", kT[:, blk, :], pb[:, 0:512], [("ps", ba)], [("a_kT", blk)])

    def state_step(self, dr, c, inter):
        ps, S, Sp, tS, cs, kT, vtm, qt = self.ps, self.S, self.Sp, self.tS, self.cs, self.kT, self.vtm, self.qt
        blk, half = c // 2, c % 2
        pr = slice(64 * half, 64 * half + 64)
        ba = self.bank()
        for hd in range(4):
            self.mm(ps[:, ba, hd * 128:(hd + 1) * 128], kT[pr, blk, hd * 128:(hd + 1) * 128],
                    vtm[pr, blk, hd * 128:(hd + 1) * 128], True, True, [("a_kT", blk), ("a_v", blk)], [("ps", ba)])

        def bc(kind):
            return cs[:, dr, kind, :, c:c + 1].to_broadcast([128, 4, 128])
        S3 = S[:, dr, :].rearrange("p (h e) -> p h e", e=128)
        kS = ("S", dr)
        kc_ = [("a_cs", dr, hd) for hd in range(4)]
        if inter:
            sb = self.spn % 2
            self.spn += 1
            self.tt("pool", Sp[:, sb, :].rearrange("p (h e) -> p h e", e=128), S3, bc(0), ALU.mult, [kS] + kc_, [("a_Sp", sb)])
            for hd in range(4):
                self.mm(ps[:, 4 + hd, c * 64:(c + 1) * 64], Sp[:, sb, hd * 128:(hd + 1) * 128],
                        qt[:, dr, hd, c * 64:(c + 1) * 64], False, dr == 1, [("a_Sp", sb), ("a_qt", dr, hd)], [("ps", 4 + hd)])
        self.tt("dve", tS.rearrange("p (h e) -> p h e", e=128), ps[:, ba, :].rearrange("p (h e) -> p h e", e=128), bc(2),
                ALU.mult, [("ps", ba)] + kc_, [("a_tS",)])
        self.tt("dve", S3, S3, bc(1), ALU.mult, [kS] + kc_, [kS])
        self.tt("pool", S[:, dr, :], S[:, dr, :], tS, ALU.add, [kS, ("a_tS",)], [kS])

    def proj_fm(self, name, nch, epilogue, ncols=T):
        ps, xb = self.ps, self.xb
        sl, wv = self.wload(name)
        wv = wv[:, 0:8 * nch * 128].rearrange("p (a b) -> p a b", b=nch * 128)
        for c in range(nch):
            ba = self.bank()
            for k in range(KC):
                self.mm(ps[:, ba, 0:ncols], wv[:, k, c * 128:(c + 1) * 128], xb[:, k, 0:ncols], k == 0, k == 7,
                        [("w", sl), ("xb", k)], [("ps", ba)])
            epilogue(c, ps[:, ba, 0:ncols], ba)

    def proj_tm(self, j):
        ps, xb, vtm = self.ps, self.xb, self.vtm
        s1, w1 = self.wload(f"e{j}_ia")
        s2, w2 = self.wload(f"e{j}_vb")
        w1 = w1.rearrange("p (a b) -> p a b", b=512)
        w2 = w2[:, 0:1024].rearrange("p (a b) -> p a b", b=128)
        for blk in range(4):
            ba, bb = self.bank(), self.bank()
            for k in range(KC):
                self.mm(ps[:, ba, :], xb[:, k, blk * 128:(blk + 1) * 128], w1[:, k, :], k == 0, k == 7,
                        [("w", s1), ("xb", k)], [("ps", ba)])
            for k in range(KC):
                self.mm(ps[:, bb, 0:128], xb[:, k, blk * 128:(blk + 1) * 128], w2[:, k, :], k == 0, k == 7,
                        [("w", s2), ("xb", k)], [("ps", bb)])
            self.cp("act", vtm[:, blk, 0:512], ps[:, ba, :], [("ps", ba)], [("a_v", blk)])
            self.cp("dve", vtm[:, blk, 512:640], ps[:, bb, 0:128], [("ps", bb)], [("a_v", blk)])

    def attention(self, j, i):
        ps, QT, KT, Vp, PT, tsc, oT, rs = self.ps, self.QT, self.KT, self.Vp, self.PT, self.tsc, self.oT, self.rs
        NT = self.NT
        om, _ = self.cindex["mdist"]
        onb, _ = self.cindex["nbias"]
        kbs = [kb for kb in range(6) if not (kb == 0 and i == 0) and not (kb == 5 and i == NT - 1)]
        nb = {}
        if i % TPU == 0 and i > 0:
            nb[0] = self.cst[:, onb + i // TPU:onb + i // TPU + 1]
        if i % TPU == TPU - 1 and i < NT - 1:
            nb[5] = self.cst[:, onb + i // TPU + 1:onb + i // TPU + 2]
        tn = 0
        for c in range(4):
            kv = c // 2
            for ab in range(2):
                hq = 2 * c + ab
                pr = slice(64 * ab, 64 * ab + 64)
                slope8 = 2.0 ** (-(hq + 1))
                for kb in kbs:
                    q0, q1 = max(0, kb - 2), min(3, kb)
                    nq = (q1 - q0 + 1) * 128
                    slot0 = q0 + 2 - kb
                    ba = self.bank()
                    self.mm(ps[:, ba, 0:nq], KT[pr, kv, kb * 128:(kb + 1) * 128], QT[pr, c, q0 * 128:q0 * 128 + nq], True, True,
                            [("a_KT", kv, kb), ("a_QT", c)], [("ps", ba)])
                    tb = tn % 2
                    tn += 1
                    self.stt("dve", tsc[:, tb, 0:nq], self.cst[:, om + slot0 * 128:om + slot0 * 128 + nq], slope8, ps[:, ba, 0:nq],
                             ALU.mult, ALU.add, [("ps", ba), ("c",)], [("a_tsc", tb)])
                    if kb in nb:
                        self.act(PT[:, ab, kb, 0:nq], tsc[:, tb, 0:nq], AF.Exp, [("a_tsc", tb), ("c",)], [("a_PT", ab, kb)],
                                 bias=nb[kb], scale=0.125)
                    else:
                        self.act(PT[:, ab, kb, 0:nq], tsc[:, tb, 0:nq], AF.Exp, [("a_tsc", tb)], [("a_PT", ab, kb)], scale=0.125)
            bo, bd = (4, 5) if c % 2 == 0 else (6, 7)
            for qb in range(4):
                terms = [(ab, kb) for ab in range(2) for kb in (qb, qb + 1, qb + 2) if kb in kbs]
                for n, (ab, kb) in enumerate(terms):
                    q0 = max(0, kb - 2)
                    rhs = PT[:, ab, kb, (qb - q0) * 128:(qb - q0 + 1) * 128]
                    self.mm(ps[:, bo, qb * 128:(qb + 1) * 128], Vp[:, kb, kv, ab, :], rhs, n == 0, n == len(terms) - 1,
                            [("a_Vp", kb), ("a_PT", ab, kb)], [("ps", bo)])
                for n, (ab, kb) in enumerate(terms):
                    q0 = max(0, kb - 2)
                    rhs = PT[:, ab, kb, (qb - q0) * 128:(qb - q0 + 1) * 128]
                    self.mm(ps[:, bd, qb * 128:(qb + 1) * 128], self.DAB[:, ab, :], rhs, n == 0, n == len(terms) - 1,
                            [("c",), ("a_PT", ab, kb)], [("ps", bd)])
            self.act(rs, ps[:, bd, :], AF.Identity, [("ps", bd), ("c",)], [("a_rs",)], bias=self.esink[:, j, c:c + 1])
            self.P.op("dve", (lambda o, i_: (lambda e: e.reciprocal(out=o, in_=i_)))(rs, rs), [("a_rs",)], [("a_rs",)])
            self.tt("dve", oT[:, 4 + c, :], ps[:, bo, :], rs, ALU.mult, [("ps", bo), ("a_rs",)], [("a_oT", 4 + c)])

    def build_vp(self, kb, src, rkeys):
        Vp = self.Vp
        for kv in range(2):
            self.cp("pool", Vp[:, kb, kv, 0, 0:64], src[:, kv * 64:(kv + 1) * 64], rkeys, [("a_Vp", kb)])
            self.cp("pool", Vp[:, kb, kv, 1, 64:128], src[:, kv * 64:(kv + 1) * 64], rkeys, [("a_Vp", kb)])

    def even_mixer(self, j, layer, i):
        ps, xb = self.ps, self.xb
        sgm, qs, gsb, QT, KT, vtm, oT = self.sgm, self.qs, self.gsb, self.QT, self.KT, self.vtm, self.oT
        NT = self.NT
        self.proj_fm(f"e{j}_ff", 4, lambda c, p, b: self.act(sgm[:, 0, c, :], p, AF.Sigmoid, [("ps", b)], [("a_sgm", 0, c)]))
        self.proj_fm(f"e{j}_fb", 4, lambda c, p, b: self.act(sgm[:, 1, c, :], p, AF.Sigmoid, [("ps", b)], [("a_sgm", 1, c)]))
        self.proj_fm(f"e{j}_qa", 4, lambda c, p, b: self.act(qs[:, c, :], p, AF.Silu, [("ps", b)], [("a_qs", c)]))
        self.proj_fm(f"e{j}_ga", 4, lambda c, p, b: self.act(gsb[:, c, :], p, AF.Silu, [("ps", b)], [("a_gs", c)]))
        self.proj_fm(f"e{j}_qb", 4, lambda c, p, b: self.cp("dve", QT[:, c, :], p, [("ps", b)], [("a_QT", c)]))
        self.proj_fm(f"e{j}_kd", 2, lambda c, p, b: self.cp("dve", KT[:, c, 128:640], p, [("ps", b)],
                                                            [("a_KT", c, kb) for kb in range(1, 5)]))
        self.proj_tm(j)
        self.memset("pool", self.Vp, 0.0, [("a_Vp", kb) for kb in range(6)])
        if i > 0:
            for kv in range(2):
                self.cp("pool", KT[:, kv, 0:128], self.KTh[:, kv, :], [("KTh",)], [("a_KT", kv, 0)])
            self.build_vp(0, self.Vh, [("Vh",)])
        if i < NT - 1:
            self.P.dma("act", self.hal, self.halo[i + 1], [("halo", i + 1)], [("a_hal",)], dsem="hal")
            for kv in range(2):
                self.cp("pool", KT[:, kv, 640:768], self.hal[:, kv * 128:(kv + 1) * 128], [("a_hal",)], [("a_KT", kv, 5)])
            self.build_vp(5, self.hal[:, 256:384], [("a_hal",)])
        for blk in range(4):
            self.build_vp(blk + 1, vtm[:, blk, 512:640], [("a_v", blk)])
        of, _ = self.cindex["flags"]
        if i % TPU == 0:
            u = i // TPU
            self.act(self.S[:, 0, :], self.S[:, 0, :], AF.Identity, [("S", 0), ("c",)], [("S", 0)],
                     scale=self.cst[:, of + u:of + u + 1])
        self.P.dma("act", self.S[:, 1, :], self.sbs[i], [("sbs", i)], [("S", 1)], dsem="sbl")
        for dr in range(2):
            for hd in range(4):
                self.gates(j, dr, hd)
        for dr in range(2):
            mask = self.trif if dr == 0 else self.trib
            for blk in range(4):
                ba = self.bank()
                cols = slice(blk * 128, (blk + 1) * 128)
                for hd in range(4):
                    self.mm(ps[:, ba, hd * 128:(hd + 1) * 128], self.kt[:, dr, hd, cols], self.qt[:, dr, hd, cols], True, True,
                            [("a_kt", dr, hd), ("a_qt", dr, hd)], [("ps", ba)])
                ab_ = self.atn % 2
                self.atn += 1
                self.tt("dve", self.AT[:, ab_, :], ps[:, ba, :], mask, ALU.mult, [("ps", ba), ("c",)], [("a_AT", ab_)])
                for hd in range(4):
                    self.mm(ps[:, 4 + hd, cols], vtm[:, blk, hd * 128:(hd + 1) * 128], self.AT[:, ab_, hd * 128:(hd + 1) * 128],
                            dr == 0 and blk == 0, False, [("a_v", blk), ("a_AT", ab_)], [("ps", 4 + hd)])
        self.ktrans(0)
        for c in range(8):
            self.state_step(0, c, True)
        self.ktrans(1)
        for c in range(7, -1, -1):
            self.state_step(1, c, True)
        for hd in range(4):
            self.act(self.sq, ps[:, 4 + hd, :], AF.Square, [("ps", 4 + hd)], [("a_sq",)])
            ba = self.bank()
            self.mm(ps[:, ba, :], self.onesb, self.sq, True, True, [("a_sq",), ("c",)], [("ps", ba)])
            self.act(self.rs, ps[:, ba, :], AF.Ln, [("ps", ba)], [("a_rs",)], bias=128.0 * RMS_EPS)
            self.act(self.rs, self.rs, AF.Exp, [("a_rs",)], [("a_rs",)], scale=-0.5)
            self.tt("dve", self.on, ps[:, 4 + hd, :], self.rs, ALU.mult, [("ps", 4 + hd), ("a_rs",)], [("a_on",)])
            self.stt("dve", oT[:, hd, :], self.on, self.ngs[:, j:j + 1], gsb[:, hd, :], ALU.mult, ALU.mult,
                     [("a_on",), ("a_gs", hd), ("c",)], [("a_oT", hd)])
        self.attention(j, i)
        import os
        dbg = os.environ.get("KDBG", "")
        if "zattn" in dbg:
            self.memset("dve", oT[:, 4:8, :], 0.0, [("a_oT", k) for k in range(4, 8)])
        if "zhgrn" in dbg:
            self.memset("dve", oT[:, 0:4, :], 0.0, [("a_oT", k) for k in range(4)])
        for kv in range(2):
            self.cp("pool", self.KTh[:, kv, :], KT[:, kv, 512:640], [("a_KT", kv, 4)], [("KTh",)])
        self.cp("pool", self.Vh, vtm[:, 3, 512:640], [("a_v", 3)], [("Vh",)])
        if "dumpo" in dbg:
            for k in range(KC):
                self.cp("dve", self.xT[:, k, :], oT[:, k, :], [("a_oT", k)], [("xT", k)])
            return
        for s in range(2):
            sl, wo = self.wload(f"e{j}_wo{s}")
            wo = wo.rearrange("p (a b) -> p a b", b=512)
            for c in range(4):
                m = 4 * s + c
                bo = self.bank()
                for k in range(KC):
                    self.mm(ps[:, bo, :], wo[:, k, c * 128:(c + 1) * 128], oT[:, k, :], k == 0, k == 7,
                            [("w", sl), ("a_oT", k)], [("ps", bo)])
                self.ln_accum(m, ps[:, bo, :], bo)
        self.ln_finish(self.lnp(0, layer), self.lnp(1, layer))

    def prepass(self, j, layer, i):
        sgm, KT, vtm = self.sgm, self.KT, self.vtm
        NT = self.NT
        self.load_x(layer, i)
        self.fence()
        self.proj_fm(f"e{j}_fb", 4, lambda c, p, b: self.act(sgm[:, 1, c, :], p, AF.Sigmoid, [("ps", b)], [("a_sgm", 1, c)]))
        self.proj_fm(f"e{j}_kd", 2, lambda c, p, b: self.cp("dve", KT[:, c, 128:256], p, [("ps", b)], [("a_KT", c, 1)]), ncols=128)
        self.proj_tm(j)
        for kv in range(2):
            self.cp("pool", self.hal[:, kv * 128:(kv + 1) * 128], KT[:, kv, 128:256], [("a_KT", kv, 1)], [("a_hal",)])
        self.cp("pool", self.hal[:, 256:384], vtm[:, 0, 512:640], [("a_v", 0)], [("a_hal",)])
        self.P.dma("act", self.halo[i], self.hal, [("a_hal",)], [("halo", i)], dsem="hals")
        of, _ = self.cindex["flags"]
        if i == NT - 1:
            self.memset("dve", self.S[:, 1, :], 0.0, [("S", 1)])
        elif (i + 1) % TPU == 0:
            u = (i + 1) // TPU
            self.act(self.S[:, 1, :], self.S[:, 1, :], AF.Identity, [("S", 1), ("c",)], [("S", 1)],
                     scale=self.cst[:, of + u:of + u + 1])
        self.P.dma("act", self.sbs[i], self.S[:, 1, :], [("S", 1)], [("sbs", i)], dsem="sbs")
        if i > 0:
            for hd in range(4):
                self.gates(j, 1, hd, need_q=False)
            self.ktrans(1)
            for c in range(7, -1, -1):
                self.state_step(1, c, False)
        self.fence()


def build_program(NU, windex, TOTW, cindex, NCST, npairs=2):
    from contextlib import ExitStack
    nc = bass.Bass("TRN2", target_bir_lowering=False)
    NT = NU * TPU
    NTOK = NU * UNIT
    xin = nc.dram_tensor("xin", [NTOK, D], F32, kind="ExternalInput").ap()
    wsrc = nc.dram_tensor("wsrc", [128, TOTW], F32, kind="ExternalInput").ap()
    cstd = nc.dram_tensor("cst", [128, NCST], F32, kind="ExternalInput").ap()
    cbsd = nc.dram_tensor("cbs", [128, 2048], F32, kind="ExternalInput").ap()
    yout = nc.dram_tensor("yout", [NTOK, D], F32, kind="ExternalOutput").ap()
    import os
    wbf = {name: nc.dram_tensor(f"wbf_{name}", [128, W], BF16, kind="Internal").ap() for name, (off, W) in windex.items()}
    xs = [nc.dram_tensor(f"xs{i}", [128, KC * T], F32, kind="Internal").ap() for i in range(NT)]
    sbs = nc.dram_tensor("sbs", [NT, 128, 512], F32, kind="Internal").ap()
    halo = nc.dram_tensor("halo", [NT, 128, 384], BF16, kind="Internal").ap()

    stack = ExitStack()
    with stack:
        NBYTES = 207 * 1024
        arena = stack.enter_context(nc.sbuf_tensor("arena", [128, NBYTES // 4], F32))
        psum = stack.enter_context(nc.psum_tensor("psum", [128, 8, 512], F32))
        P = Prog(nc)
        B = Builder(nc, P, NU, windex, cindex)
        B.xin, B.yout, B.wbf, B.xs, B.sbs, B.halo = xin, yout, wbf, xs, sbs, halo
        B.ps = psum
        B.spn = 0
        B.atn = 0
        M = Mem(arena, NBYTES)
        B.NSLOT = 3
        B.wslot = M.alloc([B.NSLOT, 4096], BF16)
        B.xT = M.alloc([KC, T], F32)
        B.xb = M.alloc([KC, T], BF16)
        B.rb = M.alloc([2, T], BF16)
        B.rsq = M.alloc([2, T], BF16)
        B.stat = M.alloc([4, T], F32)
        B.cst = M.alloc([NCST], F32)
        B.identf = B.cst[:, cindex["ident"][0]:cindex["ident"][0] + 128]
        B.scanmask = B.cst[:, cindex["scanmask"][0]:cindex["scanmask"][0] + 512]
        B.identb = M.alloc([128], BF16)
        B.onesb = M.alloc([128], BF16)
        B.DAB = M.alloc([2, 128], BF16)
        B.trif = M.alloc([512], BF16)
        B.trib = M.alloc([512], BF16)
        B.S = M.alloc([2, 512], F32)
        B.KTh = M.alloc([2, 128], BF16)
        B.Vh = M.alloc([128], BF16)
        B.wst = M.alloc([2, 8, 128], BF16)
        B.Cc = M.alloc([2, 16, 128], F32)
        B.lbt = M.alloc([2, 2, 4], F32)
        B.omlt = M.alloc([2, 2, 4], F32)
        B.nomlt = M.alloc([2, 2, 4], F32)
        B.esink = M.alloc([2, 4], F32)
        B.ngs = M.alloc([2], F32)
        B.fz = M.alloc([8], F32)
        base = M.mark()
        B.qs = M.alloc([4, T], F32)
        B.gsb = M.alloc([4, T], F32)
        B.sgm = M.alloc([2, 4, T], F32)
        B.gB = M.alloc([T], F32)
        B.gC = M.alloc([T], F32)
        B.gE = M.alloc([T], F32)
        B.qt = M.alloc([2, 4, T], BF16)
        B.kt = M.alloc([2, 4, T], BF16)
        B.kT = M.alloc([4, 512], BF16)
        B.vtm = M.alloc([4, 640], BF16)
        B.Sp = M.alloc([2, 512], BF16)
        B.tS = M.alloc([512], F32)
        B.AT = M.alloc([2, 512], BF16)
        B.oT = M.alloc([KC, T], BF16)
        B.QT = M.alloc([4, T], BF16)
        B.KT = M.alloc([2, 768], BF16)
        B.Vp = M.alloc([6, 2, 2, 128], BF16)
        B.PT = M.alloc([2, 6, 384], BF16)
        B.tsc = M.alloc([2, 384], F32)
        B.cs = M.alloc([2, 4, 4, 8], F32)
        B.sq = M.alloc([T], BF16)
        B.on = M.alloc([T], F32)
        B.rs = M.alloc([T], F32)
        B.hal = M.alloc([384], BF16)
        em_top = M.mark()
        M.reset(base)
        B.h = M.alloc([FC, T], BF16)
        B.sg = M.alloc([2, T], F32)
        M.reset(base)
        B.vtok = M.alloc([4, 2048], F32)
        B.vn = M.alloc([4, 2048], BF16)
        B.yin = M.alloc([16, T], BF16)
        B.uc = M.alloc([2, T], F32)
        B.tmp = M.alloc([2, T], F32)
        B.bst = M.alloc([4, 4, 6], F32)
        B.bmv = M.alloc([4, 2], F32)
        M.reset(base)
        B.xtok = M.alloc([4, D], F32)
        B.pc0 = M.alloc([4, D], BF16)
        B.pc1 = M.alloc([4, D], BF16)
        B.pc2 = M.alloc([4, D], BF16)
        M.reset(base)
        B.ytok = M.alloc([4, D], F32)
        M.reset(base)
        cbs_sb = M.alloc([2, 8, 128], F32)
        wtmp = M.alloc([1024], F32)

        dbg = os.environ.get("KDBG", "")
        names = list(windex.keys())
        for q, name in enumerate(names):
            a, W = windex[name]
            P.dma("pool", wbf[name], wsrc[:, a:a + W], [], [("wbfc", q)] + ([("wbf",)] if q == len(names) - 1 else []), dsem="cast")
        P.dma("sp", B.cst, cstd, [], [("c",)], dsem="cst")
        ci = cindex
        P.dma("sp", cbs_sb, cbsd.rearrange("p (l g t) -> p l g t", l=2, g=8), [], [("a_cbs",)], dsem="cst2")
        B.cp("dve", B.identb, B.identf, [("c",)], [("c",)])
        B.memset("dve", B.onesb, 1.0, [("c",)])
        B.memset("dve", B.DAB, 0.0, [("c",)])
        B.memset("dve", B.DAB[:, 0, 0:64], 1.0, [("c",)])
        B.memset("dve", B.DAB[:, 1, 64:128], 1.0, [("c",)])
        B.cp("dve", B.trif, B.cst[:, ci["trif"][0]:ci["trif"][0] + 512], [("c",)], [("c",)])
        B.cp("dve", B.trib, B.cst[:, ci["trib"][0]:ci["trib"][0] + 512], [("c",)], [("c",)])
        B.memset("dve", B.S, 0.0, [("S", 0), ("S", 1)])
        B.memset("dve", B.KTh, 0.0, [("KTh",)])
        B.memset("dve", B.Vh, 0.0, [("Vh",)])
        ol = ci["lbl"][0]
        lbl = B.cst[:, ol:ol + 16].rearrange("p (l d h) -> p l d h", l=2, d=2)
        B.memset("dve", B.lbt, 0.0, [("c",)])
        B.tt("dve", B.lbt[:, 1, :, :], lbl[:, 1, :, :], lbl[:, 0, :, :], ALU.subtract, [("c",)], [("c",)])
        B.act(B.lbt[:, 1, :, :], B.lbt[:, 1, :, :], AF.Sigmoid, [("c",)], [("c",)])
        B.ts("dve", B.omlt, B.lbt, -1.0, 1.0, ALU.mult, ALU.add, [("c",)], [("c",)])
        B.ts("dve", B.nomlt, B.omlt, -1.0, None, ALU.mult, None, [("c",)], [("c",)])
        osk = ci["sink"][0]
        B.act(B.esink, B.cst[:, osk:osk + 8].rearrange("p (l c) -> p l c", l=2), AF.Exp, [("c",)], [("c",)])
        ong = ci["ng"][0]
        B.ts("dve", B.ngs, B.cst[:, ong:ong + 2], float(np.sqrt(128.0)), None, ALU.mult, None, [("c",)], [("c",)])
        ob = ci["clnb"][0]
        for j in range(0 if "nowst" in dbg else 2):
            off, W = windex[f"o{j}_ws"]
            P.dma("sp", B.wst[:, j, :, :], wbf[f"o{j}_ws"].rearrange("p (g t) -> p g t", g=8), [("wbf",)], [("c",)], dsem=f"cst3{j}")
            for hf in range(2):
                B.mm(psum[:, hf, :], B.onesb, B.wst[:, j, 4 * hf:4 * hf + 4, :], True, True, [("c",)], [("ps", hf)])
            for fc in range(16):
                g = fc // 2
                B.stt("dve", B.Cc[:, j, fc, :], psum[:, g // 4, (g % 4) * 128:(g % 4 + 1) * 128],
                      B.cst[:, ob + j * 16 + fc:ob + j * 16 + fc + 1], cbs_sb[:, j, g, :], ALU.mult, ALU.add,
                      [("ps", g // 4), ("c",), ("a_cbs",)], [("c",)])
        B.fence()

        import os
        stages = os.environ.get("KSTAGES", "pre,em,ffn0,sgu,ffn1").split(",")
        for pair in range(npairs):
            le, lo = 2 * pair, 2 * pair + 1
            if "pre" in stages:
                for i in range(NT - 1, -1, -1):
                    B.prepass(pair, le, i)
            for i in range(NT):
                if "noloop" in dbg:
                    continue
                B.load_x(2 if "ldalt" in dbg else le, i)
                B.fence()
                if "nostore" in dbg:
                    continue
                if "em" in stages:
                    B.even_mixer(pair, le, i)
                    B.fence()
                if "ffn0" in stages:
                    B.ffn(le)
                    B.fence()
                if "sgu" in stages:
                    B.sgu(pair, lo)
                    B.fence()
                if "ffn1" in stages:
                    B.ffn(lo)
                    B.fence()
                B.store_x(3 if pair == npairs - 1 else 1, i)
                B.fence()
        P.emit(stack)
    return nc


_WKEYS = ("w_in_ab", "hgrn_lb_logits", "hgrn_norm_g", "attn_sink", "w_out_ab", "w_in_c", "c_ln_g", "c_ln_b", "c_ws",
          "c_bs", "w_out_c", "ffn_w_gate", "ffn_w_up", "ffn_w_down", "ln_mix_g", "ln_mix_b", "ln_ffn_g", "ln_ffn_b")


def run_cores(xs_per_core, flags_per_core, weights, npairs=2, trace=False):
    NU = flags_per_core[0].shape[0]
    w = {k: np.asarray(weights[k], dtype=np.float32) for k in _WKEYS}
    slabs, windex = build_slabs(w)
    wsrc = np.ascontiguousarray(np.concatenate(slabs, axis=1))
    TOTW = wsrc.shape[1]
    in_maps = []
    for x, fl in zip(xs_per_core, flags_per_core):
        cst, cindex, cbs = build_consts(w, np.asarray(fl, np.float32))
        in_maps.append({"xin": np.ascontiguousarray(x, dtype=np.float32), "wsrc": wsrc, "cst": cst, "cbs": cbs})
    nc = build_program(NU, windex, TOTW, cindex, cst.shape[1], npairs=npairs)
    res = run_bass_kernel_spmd(nc, in_maps, core_ids=list(range(len(in_maps))), trace=trace)
    return [r["yout"] for r in res.results], res


def kernel(x_prompt, x_sample, **weights):
    x_prompt = np.asarray(x_prompt, dtype=np.float32)
    x_sample = np.asarray(x_sample, dtype=np.float32)
    NU = 8
    nseq = x_prompt.shape[0]
    assign = [[] for _ in range(8)]
    for s in range(nseq):
        assign[1 + s % 7].append(s)
    xs, fls = [x_sample[0]], [np.array([0.0] + [1.0] * (NU - 1), np.float32)]
    for c in range(1, 8):
        ids = assign[c] + [assign[c][-1]] * (NU - len(assign[c]))
        xs.append(x_prompt[ids].reshape(NU * UNIT, D))
        fls.append(np.zeros(NU, np.float32))
    outs, _ = run_cores(xs, fls, weights)
    y_sample = outs[0].reshape(1, NU * UNIT, D)
    y_prompt = np.empty_like(x_prompt)
    for c in range(1, 8):
        o = outs[c].reshape(NU, UNIT, D)
        for n, s in enumerate(assign[c]):
            y_prompt[s] = o[n]
    return (y_prompt, y_sample)
```

```python
import numpy as np
import ml_dtypes
import concourse.bass as bass
import concourse.mybir as mybir
from concourse.bass_utils import run_bass_kernel_spmd

F32 = mybir.dt.float32
BF16 = mybir.dt.bfloat16
AF = mybir.ActivationFunctionType
ALU = mybir.AluOpType

D = 1024
KC = 8
T = 512
UNIT = 2048
TPU = UNIT // T
FFN = 2816
FC = 22
DEPTH = 4
ALPHA = (2.0 * DEPTH) ** 0.25
LN_EPS = 1e-5
RMS_EPS = 1e-6
NEG = -30000.0
MID = 31


class _Op:
    __slots__ = ("eng", "fn", "deps", "is_dma", "dsem", "dval", "sig", "sval", "ssem")

    def __init__(self, eng, fn, is_dma=False, dsem=None):
        self.eng = eng
        self.fn = fn
        self.deps = set()
        self.is_dma = is_dma
        self.dsem = dsem
        self.dval = 0
        self.sig = False
        self.sval = 0
        self.ssem = 0


class Prog:
    ENGS = ("pe", "act", "dve", "pool", "sp")
    SEM_ROLL = 20000

    def __init__(self, nc):
        self.nc = nc
        self.ops = []
        self.last_w = {}
        self.readers = {}
        self.dsem_tot = {}

    def _track(self, op, idx, reads, writes):
        deps = op.deps
        reads, writes = list(reads), list(writes)
        if any(k[0].startswith("a_") for k in reads + writes):
            reads.append(("arena",))
        for k in reads:
            w = self.last_w.get(k)
            if w is not None:
                deps.add(w)
        for k in writes:
            w = self.last_w.get(k)
            if w is not None:
                deps.add(w)
            for r in self.readers.get(k, ()):
                deps.add(r)
        deps.discard(idx)
        for k in reads:
            self.readers.setdefault(k, []).append(idx)
        for k in writes:
            self.last_w[k] = idx
            self.readers[k] = []

    def op(self, eng, fn, reads=(), writes=()):
        o = _Op(eng, fn)
        idx = len(self.ops)
        self.ops.append(o)
        self._track(o, idx, reads, writes)
        return idx

    def dma(self, eng, out, in_, reads=(), writes=(), dsem="d"):
        o = _Op(eng, (out, in_), is_dma=True, dsem=dsem)
        self.dsem_tot[dsem] = self.dsem_tot.get(dsem, 0) + 16
        o.dval = self.dsem_tot[dsem]
        idx = len(self.ops)
        self.ops.append(o)
        self._track(o, idx, reads, writes)
        return idx

    def emit(self, stack):
        nc = self.nc
        ops = self.ops
        for o in ops:
            for d in o.deps:
                if not ops[d].is_dma and not (o.eng == "pe" and ops[d].eng == "pe" and not o.is_dma):
                    ops[d].sig = True
        cnt = {e: 0 for e in self.ENGS}
        nsem = {e: 1 for e in self.ENGS}
        for o in ops:
            if o.sig:
                cnt[o.eng] += 1
                if cnt[o.eng] > self.SEM_ROLL:
                    cnt[o.eng] = 1
                    nsem[o.eng] += 1
                o.sval = cnt[o.eng]
                o.ssem = nsem[o.eng] - 1
        sems = {e: [stack.enter_context(nc.semaphore(f"s_{e}_{i}")) for i in range(nsem[e])] for e in self.ENGS}
        dsems = {k: stack.enter_context(nc.semaphore(f"d_{k}")) for k in self.dsem_tot}
        fin = stack.enter_context(nc.semaphore("fin"))
        block = stack.enter_context(nc.Block())
        out_dmas = [o for o in ops if o.is_dma and o.dsem.startswith("out")]
        last_out = {}
        for o in out_dmas:
            last_out[o.dsem] = o.dval

        def stream(ename):
            def body(e):
                waited = {}
                for o in ops:
                    if o.eng != ename:
                        continue
                    need = {}
                    for d in o.deps:
                        dd = ops[d]
                        if dd.is_dma:
                            key = ("d", dd.dsem)
                            val = dd.dval
                        else:
                            if dd.eng == ename and ename == "pe":
                                continue
                            key = (dd.eng, dd.ssem)
                            val = dd.sval
                        if val > need.get(key, 0):
                            need[key] = val
                    for key, val in need.items():
                        if waited.get(key, 0) >= val:
                            continue
                        waited[key] = val
                        if key[0] == "d":
                            e.wait_ge(dsems[key[1]], val)
                        else:
                            e.wait_ge(sems[key[0]][key[1]], val)
                    if o.is_dma:
                        e.dma_start(out=o.fn[0], in_=o.fn[1]).then_inc(dsems[o.dsem], 16)
                    else:
                        ins = o.fn(e)
                        if o.sig:
                            ins.then_inc(sems[ename][o.ssem], 1)
                if ename == "act":
                    for k, v in last_out.items():
                        e.wait_ge(dsems[k], v)
            return body

        block.tensor(stream("pe"))
        block.scalar(stream("act"))
        block.vector(stream("dve"))
        block.gpsimd(stream("pool"))
        block.sync(stream("sp"))


def _fm(w):
    K, N = w.shape
    return np.ascontiguousarray(w.reshape(K // 128, 128, N).transpose(1, 0, 2).reshape(128, -1))


def build_slabs(inp):
    slabs, index = [], {}
    off = 0

    def add(name, arr):
        nonlocal off
        arr = np.ascontiguousarray(arr, dtype=np.float32)
        assert arr.shape[0] == 128 and arr.shape[1] <= 4096, (name, arr.shape)
        slabs.append(arr)
        index[name] = (off, arr.shape[1])
        off += arr.shape[1]

    for j in range(2):
        w = inp["w_in_ab"][j]
        add(f"e{j}_ff", _fm(w[:, 512:1024]))
        add(f"e{j}_fb", _fm(w[:, 1024:1536]))
        add(f"e{j}_qa", _fm(w[:, 0:512]))
        add(f"e{j}_ga", _fm(w[:, 2048:2560]))
        add(f"e{j}_qb", _fm(w[:, 2560:3072]))
        kd = np.concatenate([w[:, 3072:3136], w[:, 3072:3136], w[:, 3136:3200], w[:, 3136:3200]], axis=1)
        add(f"e{j}_kd", _fm(kd))
        add(f"e{j}_ia", _fm(w[:, 1536:2048]))
        add(f"e{j}_vb", _fm(w[:, 3200:3328]))
        wo = inp["w_out_ab"][j]
        add(f"e{j}_wo0", _fm(wo[:, 0:512]))
        add(f"e{j}_wo1", _fm(wo[:, 512:1024]))
        wc = inp["w_in_c"][j]
        for s in range(4):
            add(f"o{j}_v{s}", _fm(wc[:, 2048 + 512 * s:2048 + 512 * (s + 1)]))
        for s in range(4):
            add(f"o{j}_u{s}", _fm(wc[:, 512 * s:512 * (s + 1)]))
        woc = inp["w_out_c"][j]
        for s in range(4):
            add(f"o{j}_wo{s}", _fm(woc[:, 256 * s:256 * (s + 1)]))
        add(f"o{j}_ws", inp["c_ws"][j].transpose(2, 0, 1).reshape(128, 1024))
    for l in range(4):
        g, u, d = inp["ffn_w_gate"][l], inp["ffn_w_up"][l], inp["ffn_w_down"][l]
        for s in range(6):
            add(f"f{l}_g{s}", _fm(g[:, 512 * s:min(512 * (s + 1), FFN)]))
            add(f"f{l}_u{s}", _fm(u[:, 512 * s:min(512 * (s + 1), FFN)]))
        for m in range(8):
            add(f"f{l}_d{m}", _fm(d[:, 128 * m:128 * (m + 1)]))
    return slabs, index


def build_consts(inp, flags):
    cols, index = [], {}
    off = 0

    def add(name, arr):
        nonlocal off
        arr = np.ascontiguousarray(arr, dtype=np.float32).reshape(128, -1)
        cols.append(arr)
        index[name] = (off, arr.shape[1])
        off += arr.shape[1]

    add("ident", np.eye(128, dtype=np.float32))
    s = np.arange(128)[:, None]
    t = np.arange(128)[None, :]
    same = (s // 64) == (t // 64)
    add("trif", np.tile((same & (s <= t)).astype(np.float32), (1, 4)))
    add("trib", np.tile((same & (s >= t)).astype(np.float32), (1, 4)))
    sm = np.ones((128, 512), np.float32)
    sm[:, ::64] = 0.0
    add("scanmask", sm)
    k = np.arange(128)[:, None]
    q = np.arange(384)[None, :]
    dist = np.abs(q - 128 - k).astype(np.float32)
    add("mdist", -8.0 * np.where(dist <= 128, dist, 1.0e6))
    nu = flags.shape[0]
    add("flags", np.tile(flags[None, :], (128, 1)))
    add("nbias", np.tile(((flags - 1.0) * 30000.0)[None, :], (128, 1)))
    lnp = np.stack([inp["ln_mix_g"], inp["ln_mix_b"], inp["ln_ffn_g"], inp["ln_ffn_b"]], 0)
    add("lnp", lnp.reshape(4, 4, 8, 128).transpose(3, 0, 1, 2))
    add("ng", inp["hgrn_norm_g"].T)
    add("lbl", inp["hgrn_lb_logits"].reshape(2, 2, 4, 128).transpose(3, 0, 1, 2))
    sk = inp["attn_sink"].reshape(2, 4, 2)
    add("sink", np.repeat(sk.transpose(2, 0, 1)[:, None], 64, axis=1).reshape(128, 8))
    add("clng", inp["c_ln_g"].reshape(2, 16, 128).transpose(2, 0, 1))
    add("clnb", inp["c_ln_b"].reshape(2, 16, 128).transpose(2, 0, 1))
    cbs = np.ascontiguousarray(np.tile(inp["c_bs"].reshape(1, 2 * 1024), (128, 1)), dtype=np.float32)
    return np.concatenate(cols, axis=1), index, cbs


class Mem:
    def __init__(self, arena, nbytes):
        self.arena = arena
        self.nbytes = nbytes
        self.top = 0

    def mark(self):
        return self.top

    def reset(self, m):
        self.top = m

    def alloc(self, free_shape, dtype, at=None):
        esz = 4 if dtype == F32 else 2
        n = int(np.prod(free_shape))
        nb = (n * esz + 31) // 32 * 32
        off = self.top if at is None else at
        assert off + nb <= self.nbytes, ("SBUF map overflow", off + nb, self.nbytes)
        if at is None:
            self.top = off + nb
        v = self.arena[:, off // 4:(off + nb) // 4]
        if dtype != F32:
            v = v.bitcast(dtype)
        v = v[:, 0:n]
        if len(free_shape) == 2:
            v = v.rearrange("p (a b) -> p a b", b=free_shape[1])
        elif len(free_shape) == 3:
            v = v.rearrange("p (a b c) -> p a b c", b=free_shape[1], c=free_shape[2])
        elif len(free_shape) == 4:
            v = v.rearrange("p (a b c d) -> p a b c d", b=free_shape[1], c=free_shape[2], d=free_shape[3])
        return v


class Builder:
    def __init__(self, nc, P, NU, windex, cindex):
        self.nc, self.P, self.NU = nc, P, NU
        self.NT = NU * TPU
        self.windex, self.cindex = windex, cindex
        self.psn = 0
        self.wsn = 0
        self.fence_n = 0

    def mm(self, out, lhsT, rhs, start, stop, r, w):
        return self.P.op("pe", lambda e: e.matmul(out, lhsT=lhsT, rhs=rhs, start=start, stop=stop,
                                                  skip_group_check=True), r, w)

    def tr(self, out, in_, ident, r, w):
        return self.P.op("pe", lambda e: e.transpose(out, in_, ident), r, w)

    def act(self, out, in_, func, r, w, bias=None, scale=None):
        kw = {}
        if bias is not None:
            kw["bias"] = bias
        if scale is not None:
            kw["scale"] = scale
        return self.P.op("act", lambda e: e.activation(out=out, in_=in_, func=func, **kw), r, w)

    def tt(self, eng, out, in0, in1, op, r, w):
        return self.P.op(eng, lambda e: e.tensor_tensor(out=out, in0=in0, in1=in1, op=op), r, w)

    def ts(self, eng, out, in0, s1, s2, op0, op1, r, w):
        if s2 is None:
            return self.P.op(eng, lambda e: e.tensor_scalar(out=out, in0=in0, scalar1=s1, scalar2=None, op0=op0), r, w)
        return self.P.op(eng, lambda e: e.tensor_scalar(out=out, in0=in0, scalar1=s1, scalar2=s2, op0=op0, op1=op1), r, w)

    def stt(self, eng, out, in0, scalar, in1, op0, op1, r, w):
        return self.P.op(eng, lambda e: e.scalar_tensor_tensor(out=out, in0=in0, scalar=scalar, in1=in1,
                                                               op0=op0, op1=op1), r, w)

    def cp(self, eng, out, in_, r, w):
        if eng == "act":
            return self.P.op("act", lambda e: e.activation(out=out, in_=in_, func=AF.Copy), r, w)
        return self.P.op(eng, lambda e: e.tensor_copy(out=out, in_=in_), r, w)

    def memset(self, eng, ap, val, w):
        return self.P.op(eng, lambda e: e.memset(ap, val), (), w)

    def bank(self):
        b = self.psn % 4
        self.psn += 1
        return b

    def fence(self):
        self.fence_n += 1
        t = self.fz
        self.P.op("dve", lambda e: e.memset(t, 0.0), (), [("arena",)])

    def wload(self, name):
        off, W = self.windex[name]
        slot = self.wsn % self.NSLOT
        self.wsn += 1
        dst = self.wslot[:, slot, 0:W]
        self.P.dma("sp", dst, self.wbf[name], [("wbf",)], [("w", slot)], dsem=f"w{slot}")
        return slot, self.wslot[:, slot, :]

    def ln_accum(self, m, ypsum, ybank):
        xT, rb, rsq, ps = self.xT, self.rb, self.rsq, self.ps
        self.stt("dve", xT[:, m, :], xT[:, m, :], ALPHA, ypsum, ALU.mult, ALU.add,
                 [("xT", m), ("ps", ybank)], [("xT", m)])
        self.cp("pool", rb[:, m % 2, :], xT[:, m, :], [("xT", m)], [("rb", m % 2)])
        self.act(rsq[:, m % 2, :], xT[:, m, :], AF.Square, [("xT", m)], [("rsq", m % 2)])
        if m > 0:
            self.ln_stats(m - 1)
        if m == 7:
            self.ln_stats(7)

    def ln_stats(self, m):
        rb, rsq, ps = self.rb, self.rsq, self.ps
        self.mm(ps[:, 4, :], self.onesb, rb[:, m % 2, :], m == 0, m == 7, [("rb", m % 2), ("c",)], [("ps", 4)])
        self.mm(ps[:, 5, :], self.onesb, rsq[:, m % 2, :], m == 0, m == 7, [("rsq", m % 2), ("c",)], [("ps", 5)])

    def ln_finish(self, g, b, want_bf=True):
        xT, xb, st, ps = self.xT, self.xb, self.stat, self.ps
        self.ts("dve", st[:, 0, :], ps[:, 4, :], 1.0 / D, None, ALU.mult, None, [("ps", 4)], [("st", 0)])
        self.tt("dve", st[:, 1, :], st[:, 0, :], st[:, 0, :], ALU.mult, [("st", 0)], [("st", 1)])
        self.stt("dve", st[:, 2, :], ps[:, 5, :], 1.0 / D, st[:, 1, :], ALU.mult, ALU.subtract,
                 [("ps", 5), ("st", 1)], [("st", 2)])
        self.act(st[:, 3, :], st[:, 2, :], AF.Ln, [("st", 2)], [("st", 3)], bias=LN_EPS)
        self.act(st[:, 3, :], st[:, 3, :], AF.Exp, [("st", 3)], [("st", 3)], scale=-0.5)
        for m in range(KC):
            self.tt("dve", xT[:, m, :], xT[:, m, :], st[:, 0, :], ALU.subtract, [("xT", m), ("st", 0)], [("xT", m)])
        for m in range(KC):
            self.tt("pool", xT[:, m, :], xT[:, m, :], st[:, 3, :], ALU.mult, [("xT", m), ("st", 3)], [("xT", m)])
        for m in range(KC):
            self.act(xT[:, m, :], xT[:, m, :], AF.Identity, [("xT", m), ("c",)], [("xT", m)],
                     bias=b[:, m:m + 1], scale=g[:, m:m + 1])
        if want_bf:
            for m in range(KC):
                self.cp("pool", xb[:, m, :], xT[:, m, :], [("xT", m)], [("xb", m)])

    def lnp(self, kind, layer):
        o, _ = self.cindex["lnp"]
        base = o + (kind * 4 + layer) * 8
        return self.cst[:, base:base + 8]

    def ffn(self, l):
        ps, h, sg, xb = self.ps, self.h, self.sg, self.xb
        for s in range(6):
            ncs = 4 if s < 5 else 2
            gs_, gv = self.wload(f"f{l}_g{s}")
            us_, uv = self.wload(f"f{l}_u{s}")
            gv = gv[:, 0:8 * ncs * 128].rearrange("p (a b) -> p a b", b=ncs * 128)
            uv = uv[:, 0:8 * ncs * 128].rearrange("p (a b) -> p a b", b=ncs * 128)
            for c in range(ncs):
                j = 4 * s + c
                ba, bb = self.bank(), self.bank()
                for k in range(KC):
                    self.mm(ps[:, ba, :], gv[:, k, c * 128:(c + 1) * 128], xb[:, k, :], k == 0, k == 7,
                            [("w", gs_), ("xb", k)], [("ps", ba)])
                for k in range(KC):
                    self.mm(ps[:, bb, :], uv[:, k, c * 128:(c + 1) * 128], xb[:, k, :], k == 0, k == 7,
                            [("w", us_), ("xb", k)], [("ps", bb)])
                self.act(sg[:, j % 2, :], ps[:, ba, :], AF.Silu, [("ps", ba)], [("a_sg", j % 2)])
                self.tt("dve", h[:, j, :], sg[:, j % 2, :], ps[:, bb, :], ALU.mult,
                        [("a_sg", j % 2), ("ps", bb)], [("a_h", j)])
        for m in range(KC):
            ds_, dv = self.wload(f"f{l}_d{m}")
            dv = dv[:, 0:FC * 128].rearrange("p (a b) -> p a b", b=128)
            bo = self.bank()
            for k in range(FC):
                self.mm(ps[:, bo, :], dv[:, k, :], h[:, k, :], k == 0, k == FC - 1,
                        [("w", ds_), ("a_h", k)], [("ps", bo)])
            self.ln_accum(m, ps[:, bo, :], bo)
        self.ln_finish(self.lnp(2, l), self.lnp(3, l))

    def sgu(self, j, layer):
        ps, xb = self.ps, self.xb
        vtok, vn, yin, uc, tmp, bst = self.vtok, self.vn, self.yin, self.uc, self.tmp, self.bst
        ci = self.cindex
        for s in range(4):
            ws_, wv = self.wload(f"o{j}_v{s}")
            wv = wv.rearrange("p (a b) -> p a b", b=512)
            for blk in range(4):
                ba = self.bank()
                for k in range(KC):
                    self.mm(ps[:, ba, :], xb[:, k, blk * 128:(blk + 1) * 128], wv[:, k, :], k == 0, k == 7,
                            [("w", ws_), ("xb", k)], [("ps", ba)])
                self.act(vtok[:, blk, s * 512:(s + 1) * 512], ps[:, ba, :], AF.Gelu, [("ps", ba)], [("a_vtok", blk, s)])
        import os
        dbg = os.environ.get("KDBG", "")
        if "s1" in dbg:
            return
        for blk in range(4):
            for s in range(4):
                self.P.op("dve", (lambda o, i: (lambda e: e.bn_stats(out=o, in_=i)))(bst[:, blk, s, :], vtok[:, blk, s * 512:(s + 1) * 512]),
                          [("a_vtok", blk, s)], [("a_bst", blk, s)])
            mv = self.bmv[:, blk, :]
            self.P.op("dve", (lambda o, i: (lambda e: e.bn_aggr(out=o, in_=i)))(mv, bst[:, blk, :, :]),
                      [("a_bst", blk, s) for s in range(4)], [("a_bmv", blk)])
            self.act(mv[:, 1:2], mv[:, 1:2], AF.Ln, [("a_bmv", blk)], [("a_bmv", blk)], bias=LN_EPS)
            self.act(mv[:, 1:2], mv[:, 1:2], AF.Exp, [("a_bmv", blk)], [("a_bmv", blk)], scale=-0.5)
            self.stt("dve", mv[:, 0:1], mv[:, 0:1], -1.0, mv[:, 1:2], ALU.mult, ALU.mult, [("a_bmv", blk)], [("a_bmv", blk)])
            self.ts("dve", vn[:, blk, :], vtok[:, blk, :], mv[:, 1:2], mv[:, 0:1], ALU.mult, ALU.add,
                    [("a_bmv", blk)] + [("a_vtok", blk, s) for s in range(4)], [("a_vn", blk)])
        og, _ = ci["clng"]
        if "s2" in dbg:
            return
        for s in range(4):
            ws_, wu = self.wload(f"o{j}_u{s}")
            wu = wu.rearrange("p (a b) -> p a b", b=512)
            for c in range(4):
                fc = 4 * s + c
                g = fc // 2
                ba = self.bank()
                for k in range(KC):
                    self.mm(ps[:, ba, :], wu[:, k, c * 128:(c + 1) * 128], xb[:, k, :], k == 0, k == 7,
                            [("w", ws_), ("xb", k)], [("ps", ba)])
                self.act(uc[:, fc % 2, :], ps[:, ba, :], AF.Gelu, [("ps", ba)], [("a_uc", fc % 2)])
                b2 = 4 + fc % 4
                for blk in range(4):
                    self.mm(ps[:, b2, blk * 128:(blk + 1) * 128], vn[:, blk, fc * 128:(fc + 1) * 128], self.wst[:, j, g, :],
                            True, True, [("a_vn", blk), ("c",)], [("ps", b2)])
                for blk in range(4):
                    self.stt("dve", tmp[:, fc % 2, blk * 128:(blk + 1) * 128], ps[:, b2, blk * 128:(blk + 1) * 128],
                             self.cst[:, og + j * 16 + fc:og + j * 16 + fc + 1], self.Cc[:, j, fc, :], ALU.mult, ALU.add,
                             [("ps", b2), ("c",)], [("a_tmp", fc % 2, blk)])
                self.tt("pool", yin[:, fc, :], tmp[:, fc % 2, :], uc[:, fc % 2, :], ALU.mult,
                        [("a_tmp", fc % 2, blk) for blk in range(4)] + [("a_uc", fc % 2)], [("a_yin", fc)])
        if "s3" in dbg:
            return
        for s in range(4):
            ws_, wo = self.wload(f"o{j}_wo{s}")
            wo = wo.rearrange("p (a b) -> p a b", b=256)
            for c in range(2):
                m = 2 * s + c
                bo = self.bank()
                for k in range(16):
                    self.mm(ps[:, bo, :], wo[:, k, c * 128:(c + 1) * 128], yin[:, k, :], k == 0, k == 15,
                            [("w", ws_), ("a_yin", k)], [("ps", bo)])
                self.ln_accum(m, ps[:, bo, :], bo)
        self.ln_finish(self.lnp(0, layer), self.lnp(1, layer))

    def load_x(self, layer, i):
        ps, xT, xb = self.ps, self.xT, self.xb
        if layer == 0:
            xtok = self.xtok
            src = self.xin[i * T:(i + 1) * T, :].rearrange("(b p) f -> p b f", p=128)
            self.P.dma("act", xtok, src, [], [("a_xtok",)], dsem="xtok")
            import os
            dbg = os.environ.get("KDBG", "")
            if "v3" in dbg:
                return
            pcs = self.split3(xtok, [("a_xtok",)])
            if "v4" in dbg:
                return
            for k in range(KC):
                ba = self.bank()
                for blk in range(4):
                    for n in range(3):
                        self.mm(ps[:, ba, blk * 128:(blk + 1) * 128], pcs[n][:, blk, k * 128:(k + 1) * 128], self.identb,
                                n == 0, n == 2, [("a_pc", n), ("c",)], [("ps", ba)])
                if "v2" not in dbg:
                    self.cp("dve", xT[:, k, :], ps[:, ba, :], [("ps", ba)], [("xT", k)])
                self.cp("act", xb[:, k, :], xT[:, k, :], [("xT", k)], [("xb", k)])
        else:
            self.P.dma("act", xT, self.xs[i].rearrange("p (a b) -> p a b", b=T), [("xs", i)],
                       [("xT", k) for k in range(KC)], dsem="xTl")
            for k in range(KC):
                self.cp("pool", xb[:, k, :], xT[:, k, :], [("xT", k)], [("xb", k)])

    def split3(self, src, rkeys):
        pcs = [self.pc0, self.pc1, self.pc2]
        for n in range(3):
            self.cp("dve", pcs[n], src, rkeys + [("a_spl",)], [("a_pc", n)])
            if n < 2:
                self.tt("pool", src, src, pcs[n], ALU.subtract, rkeys + [("a_pc", n), ("a_spl",)], [("a_spl",)] + rkeys)
        return pcs

    def store_x(self, layer, i):
        ps, xT = self.ps, self.xT
        if layer == 1:
            self.P.dma("act", self.xs[i].rearrange("p (a b) -> p a b", b=T), xT, [("xT", k) for k in range(KC)],
                       [("xs", i)], dsem="xTs")
        else:
            ytok = self.ytok
            xk = [("xT", k) for k in range(KC)]
            pcs = [self.pc0, self.pc1, self.pc2]
            for n in range(3):
                for k in range(KC):
                    x3 = xT[:, k, :].rearrange("p (b t) -> p b t", t=128)
                    self.cp("dve", pcs[n][:, :, k * 128:(k + 1) * 128], x3, [("xT", k)], [("a_pc", n)])
                    if n < 2:
                        self.tt("pool", x3, x3, pcs[n][:, :, k * 128:(k + 1) * 128], ALU.subtract, [("xT", k), ("a_pc", n)], [("xT", k)])
            for blk in range(4):
                for half in range(2):
                    ba = self.bank()
                    for kk in range(4):
                        k = half * 4 + kk
                        for n in range(3):
                            self.mm(ps[:, ba, kk * 128:(kk + 1) * 128], pcs[n][:, blk, k * 128:(k + 1) * 128], self.identb,
                                    n == 0, n == 2, [("a_pc", n), ("c",)], [("ps", ba)])
                    self.cp("dve" if half == 0 else "act", ytok[:, blk, half * 512:(half + 1) * 512], ps[:, ba, :],
                            [("ps", ba)], [("a_ytok", blk, half)])
            dst = self.yout[i * T:(i + 1) * T, :].rearrange("(b p) f -> p b f", p=128)
            self.P.dma("act", dst, ytok, [("a_ytok", blk, hf) for blk in range(4) for hf in range(2)], [("yout", i)],
                       dsem="outy")

    def gates(self, j, dr, hd, need_q=True):
        sgm, qs, qt, kt, cs = self.sgm, self.qs, self.qt, self.kt, self.cs
        Bt, Ct, Et = self.gB, self.gC, self.gE
        lb = self.lbt[:, j, dr, hd:hd + 1]
        oml = self.omlt[:, j, dr, hd:hd + 1]
        noml = self.nomlt[:, j, dr, hd:hd + 1]
        S = ("a_sgm", dr, hd)
        kB, kC, kE = ("a_gB",), ("a_gC",), ("a_gE",)
        kcs = ("a_cs", dr, hd)
        self.act(Bt, sgm[:, dr, hd, :], AF.Ln, [S, ("c",)], [kB], bias=lb, scale=oml)
        self.ts("dve", sgm[:, dr, hd, :], sgm[:, dr, hd, :], noml, oml, ALU.mult, ALU.add, [S, ("c",)], [S])
        self.P.op("dve", (lambda o, d0, d1: (lambda e: e.tensor_tensor_scan(out=o, data0=d0, data1=d1, initial=0.0,
                                                                              op0=ALU.mult, op1=ALU.add)))(Ct, self.scanmask, Bt),
                  [kB, ("c",)], [kC])
        mid = cs[:, dr, 3, hd, :]
        if dr == 0:
            X, kX = Ct, kC
        else:
            self.tt("dve", Bt, Bt, Ct, ALU.subtract, [kB, kC], [kB])
            X, kX = Bt, kB
        X3 = X.rearrange("p (c t) -> p c t", t=64)
        self.cp("dve", mid, X3[:, :, MID], [kX], [kcs])
        self.tt("dve", X3, X3, mid.unsqueeze(2).to_broadcast([128, 8, 64]), ALU.subtract, [kX, kcs], [kX])
        self.act(Et, X, AF.Exp, [kX], [kE])
        E3 = Et.rearrange("p (c t) -> p c t", t=64)
        w0, w1, w2 = cs[:, dr, 0, hd, :], cs[:, dr, 1, hd, :], cs[:, dr, 2, hd, :]
        if dr == 0:
            self.act(w0, mid, AF.Exp, [kcs], [kcs])
            self.cp("dve", w2, E3[:, :, 63], [kE], [kcs])
            self.tt("dve", w1, w0, w2, ALU.mult, [kcs], [kcs])
        else:
            C3 = Ct.rearrange("p (c t) -> p c t", t=64)
            self.act(w1, C3[:, :, 63], AF.Exp, [kC], [kcs])
            self.act(w0, mid, AF.Exp, [kcs], [kcs])
            self.tt("dve", w0, w0, w1, ALU.mult, [kcs], [kcs])
            self.cp("dve", w2, E3[:, :, 0], [kE], [kcs])
        if need_q:
            self.tt("dve", qt[:, dr, hd, :], qs[:, hd, :], Et, ALU.mult, [("a_qs", hd), kE], [("a_qt", dr, hd)])
        self.act(X, X, AF.Exp, [kX], [kX], scale=-1.0)
        self.tt("pool", kt[:, dr, hd, :], sgm[:, dr, hd, :], X, ALU.mult, [S, kX], [("a_kt", dr, hd)])

    def ktrans(self, dr):
        ps, kt, kT = self.ps, self.kt, self.kT
        for blk in range(4):
            ba = self.bank()
            pb = ps[:, ba, :].bitcast(BF16)
            for hd in range(4):
                self.tr(pb[:, hd * 128:(hd + 1) * 128], kt[:, dr, hd, blk * 128:(blk + 1) * 128], self.identb,
                        [("a_kt", dr, hd), ("c",)], [("ps", ba)])
            self.cp("act", kT[:, blk, :], pb[:, 0:512], [("ps", ba)], [("a_kT", blk)])

    def state_step(self, dr, c, inter):
        ps, S, Sp, tS, cs, kT, vtm, qt = self.ps, self.S, self.Sp, self.tS, self.cs, self.kT, self.vtm, self.qt
        blk, half = c // 2, c % 2
        pr = slice(64 * half, 64 * half + 64)
        ba = self.bank()
        for hd in range(4):
            self.mm(ps[:, ba, hd * 128:(hd + 1) * 128], kT[pr, blk, hd * 128:(hd + 1) * 128],
                    vtm[pr, blk, hd * 128:(hd + 1) * 128], True, True, [("a_kT", blk), ("a_v", blk)], [("ps", ba)])

        def bc(kind):
            return cs[:, dr, kind, :, c:c + 1].to_broadcast([128, 4, 128])
        S3 = S[:, dr, :].rearrange("p (h e) -> p h e", e=128)
        kS = ("S", dr)
        kc_ = [("a_cs", dr, hd) for hd in range(4)]
        if inter:
            sb = self.spn % 2
            self.spn += 1
            self.tt("pool", Sp[:, sb, :].rearrange("p (h e) -> p h e", e=128), S3, bc(0), ALU.mult, [kS] + kc_, [("a_Sp", sb)])
            for hd in range(4):
                self.mm(ps[:, 4 + hd, c * 64:(c + 1) * 64], Sp[:, sb, hd * 128:(hd + 1) * 128],
                        qt[:, dr, hd, c * 64:(c + 1) * 64], False, dr == 1, [("a_Sp", sb), ("a_qt", dr, hd)], [("ps", 4 + hd)])
        self.tt("dve", tS.rearrange("p (h e) -> p h e", e=128), ps[:, ba, :].rearrange("p (h e) -> p h e", e=128), bc(2),
                ALU.mult, [("ps", ba)] + kc_, [("a_tS",)])
        self.tt("dve", S3, S3, bc(1), ALU.mult, [kS] + kc_, [kS])
        self.tt("pool", S[:, dr, :], S[:, dr, :], tS, ALU.add, [kS, ("a_tS",)], [kS])

    def proj_fm(self, name, nch, epilogue, ncols=T):
        ps, xb = self.ps, self.xb
        sl, wv = self.wload(name)
        wv = wv[:, 0:8 * nch * 128].rearrange("p (a b) -> p a b", b=nch * 128)
        for c in range(nch):
            ba = self.bank()
            for k in range(KC):
                self.mm(ps[:, ba, 0:ncols], wv[:, k, c * 128:(c + 1) * 128], xb[:, k, 0:ncols], k == 0, k == 7,
                        [("w", sl), ("xb", k)], [("ps", ba)])
            epilogue(c, ps[:, ba, 0:ncols], ba)

    def proj_tm(self, j):
        ps, xb, vtm = self.ps, self.xb, self.vtm
        s1, w1 = self.wload(f"e{j}_ia")
        s2, w2 = self.wload(f"e{j}_vb")
        w1 = w1.rearrange("p (a b) -> p a b", b=512)
        w2 = w2[:, 0:1024].rearrange("p (a b) -> p a b", b=128)
        for blk in range(4):
            ba, bb = self.bank(), self.bank()
            for k in range(KC):
                self.mm(ps[:, ba, :], xb[:, k, blk * 128:(blk + 1) * 128], w1[:, k, :], k == 0, k == 7,
                        [("w", s1), ("xb", k)], [("ps", ba)])
            for k in range(KC):
                self.mm(ps[:, bb, 0:128], xb[:, k, blk * 128:(blk + 1) * 128], w2[:, k, :], k == 0, k == 7,
                        [("w", s2), ("xb", k)], [("ps", bb)])
            self.cp("act", vtm[:, blk, 0:512], ps[:, ba, :], [("ps", ba)], [("a_v", blk)])
            self.cp("dve", vtm[:, blk, 512:640], ps[:, bb, 0:128], [("ps", bb)], [("a_v", blk)])

    def attention(self, j, i):
        ps, QT, KT, Vp, PT, tsc, oT, rs = self.ps, self.QT, self.KT, self.Vp, self.PT, self.tsc, self.oT, self.rs
        NT = self.NT
        om, _ = self.cindex["mdist"]
        onb, _ = self.cindex["nbias"]
        kbs = [kb for kb in range(6) if not (kb == 0 and i == 0) and not (kb == 5 and i == NT - 1)]
        nb = {}
        if i % TPU == 0 and i > 0:
            nb[0] = self.cst[:, onb + i // TPU:onb + i // TPU + 1]
        if i % TPU == TPU - 1 and i < NT - 1:
            nb[5] = self.cst[:, onb + i // TPU + 1:onb + i // TPU + 2]
        tn = 0
        for c in range(4):
            kv = c // 2
            for ab in range(2):
                hq = 2 * c + ab
                pr = slice(64 * ab, 64 * ab + 64)
                slope8 = 2.0 ** (-(hq + 1))
                for kb in kbs:
                    q0, q1 = max(0, kb - 2), min(3, kb)
                    nq = (q1 - q0 + 1) * 128
                    slot0 = q0 + 2 - kb
                    ba = self.bank()
                    self.mm(ps[:, ba, 0:nq], KT[pr, kv, kb * 128:(kb + 1) * 128], QT[pr, c, q0 * 128:q0 * 128 + nq], True, True,
                            [("a_KT", kv, kb), ("a_QT", c)], [("ps", ba)])
                    tb = tn % 2
                    tn += 1
                    self.stt("dve", tsc[:, tb, 0:nq], self.cst[:, om + slot0 * 128:om + slot0 * 128 + nq], slope8, ps[:, ba, 0:nq],
                             ALU.mult, ALU.add, [("ps", ba), ("c",)], [("a_tsc", tb)])
                    if kb in nb:
                        self.act(PT[:, ab, kb, 0:nq], tsc[:, tb, 0:nq], AF.Exp, [("a_tsc", tb), ("c",)], [("a_PT", ab, kb)],
                                 bias=nb[kb], scale=0.125)
                    else:
                        self.act(PT[:, ab, kb, 0:nq], tsc[:, tb, 0:nq], AF.Exp, [("a_tsc", tb)], [("a_PT", ab, kb)], scale=0.125)
            bo, bd = (4, 5) if c % 2 == 0 else (6, 7)
            for qb in range(4):
                terms = [(ab, kb) for ab in range(2) for kb in (qb, qb + 1, qb + 2) if kb in kbs]
                for n, (ab, kb) in enumerate(terms):
                    q0 = max(0, kb - 2)
                    rhs = PT[:, ab, kb, (qb - q0) * 128:(qb - q0 + 1) * 128]
                    self.mm(ps[:, bo, qb * 128:(qb + 1) * 128], Vp[:, kb, kv, ab, :], rhs, n == 0, n == len(terms) - 1,
                            [("a_Vp", kb), ("a_PT", ab, kb)], [("ps", bo)])
                for n, (ab, kb) in enumerate(terms):
                    q0 = max(0, kb - 2)
                    rhs = PT[:, ab, kb, (qb - q0) * 128:(qb - q0 + 1) * 128]
                    self.mm(ps[:, bd, qb * 128:(qb + 1) * 128], self.DAB[:, ab, :], rhs, n == 0, n == len(terms) - 1,
                            [("c",), ("a_PT", ab, kb)], [("ps", bd)])
            self.act(rs, ps[:, bd, :], AF.Identity, [("ps", bd), ("c",)], [("a_rs",)], bias=self.esink[:, j, c:c + 1])
            self.P.op("dve", (lambda o, i_: (lambda e: e.reciprocal(out=o, in_=i_)))(rs, rs), [("a_rs",)], [("a_rs",)])
            self.tt("dve", oT[:, 4 + c, :], ps[:, bo, :], rs, ALU.mult, [("ps", bo), ("a_rs",)], [("a_oT", 4 + c)])

    def build_vp(self, kb, src, rkeys):
        Vp = self.Vp
        for kv in range(2):
            self.cp("pool", Vp[:, kb, kv, 0, 0:64], src[:, kv * 64:(kv + 1) * 64], rkeys, [("a_Vp", kb)])
            self.cp("pool", Vp[:, kb, kv, 1, 64:128], src[:, kv * 64:(kv + 1) * 64], rkeys, [("a_Vp", kb)])

    def even_mixer(self, j, layer, i):
        ps, xb = self.ps, self.xb
        sgm, qs, gsb, QT, KT, vtm, oT = self.sgm, self.qs, self.gsb, self.QT, self.KT, self.vtm, self.oT
        NT = self.NT
        self.proj_fm(f"e{j}_ff", 4, lambda c, p, b: self.act(sgm[:, 0, c, :], p, AF.Sigmoid, [("ps", b)], [("a_sgm", 0, c)]))
        self.proj_fm(f"e{j}_fb", 4, lambda c, p, b: self.act(sgm[:, 1, c, :], p, AF.Sigmoid, [("ps", b)], [("a_sgm", 1, c)]))
        self.proj_fm(f"e{j}_qa", 4, lambda c, p, b: self.act(qs[:, c, :], p, AF.Silu, [("ps", b)], [("a_qs", c)]))
        self.proj_fm(f"e{j}_ga", 4, lambda c, p, b: self.act(gsb[:, c, :], p, AF.Silu, [("ps", b)], [("a_gs", c)]))
        self.proj_fm(f"e{j}_qb", 4, lambda c, p, b: self.cp("dve", QT[:, c, :], p, [("ps", b)], [("a_QT", c)]))
        self.proj_fm(f"e{j}_kd", 2, lambda c, p, b: self.cp("dve", KT[:, c, 128:640], p, [("ps", b)],
                                                            [("a_KT", c, kb) for kb in range(1, 5)]))
        self.proj_tm(j)
        self.memset("pool", self.Vp, 0.0, [("a_Vp", kb) for kb in range(6)])
        if i > 0:
            for kv in range(2):
                self.cp("pool", KT[:, kv, 0:128], self.KTh[:, kv, :], [("KTh",)], [("a_KT", kv, 0)])
            self.build_vp(0, self.Vh, [("Vh",)])
        if i < NT - 1:
            self.P.dma("act", self.hal, self.halo[i + 1], [("halo", i + 1)], [("a_hal",)], dsem="hal")
            for kv in range(2):
                self.cp("pool", KT[:, kv, 640:768], self.hal[:, kv * 128:(kv + 1) * 128], [("a_hal",)], [("a_KT", kv, 5)])
            self.build_vp(5, self.hal[:, 256:384], [("a_hal",)])
        for blk in range(4):
            self.build_vp(blk + 1, vtm[:, blk, 512:640], [("a_v", blk)])
        of, _ = self.cindex["flags"]
        if i % TPU == 0:
            u = i // TPU
            self.act(self.S[:, 0, :], self.S[:, 0, :], AF.Identity, [("S", 0), ("c",)], [("S", 0)],
                     scale=self.cst[:, of + u:of + u + 1])
        self.P.dma("act", self.S[:, 1, :], self.sbs[i], [("sbs", i)], [("S", 1)], dsem="sbl")
        for dr in range(2):
            for hd in range(4):
                self.gates(j, dr, hd)
        for dr in range(2):
            mask = self.trif if dr == 0 else self.trib
            for blk in range(4):
                ba = self.bank()
                cols = slice(blk * 128, (blk + 1) * 128)
                for hd in range(4):
                    self.mm(ps[:, ba, hd * 128:(hd + 1) * 128], self.kt[:, dr, hd, cols], self.qt[:, dr, hd, cols], True, True,
                            [("a_kt", dr, hd), ("a_qt", dr, hd)], [("ps", ba)])
                ab_ = self.atn % 2
                self.atn += 1
                self.tt("dve", self.AT[:, ab_, :], ps[:, ba, :], mask, ALU.mult, [("ps", ba), ("c",)], [("a_AT", ab_)])
                for hd in range(4):
                    self.mm(ps[:, 4 + hd, cols], vtm[:, blk, hd * 128:(hd + 1) * 128], self.AT[:, ab_, hd * 128:(hd + 1) * 128],
                            dr == 0 and blk == 0, False, [("a_v", blk), ("a_AT", ab_)], [("ps", 4 + hd)])
        self.ktrans(0)
        for c in range(8):
            self.state_step(0, c, True)
        self.ktrans(1)
        for c in range(7, -1, -1):
            self.state_step(1, c, True)
        for hd in range(4):
            self.act(self.sq, ps[:, 4 + hd, :], AF.Square, [("ps", 4 + hd)], [("a_sq",)])
            ba = self.bank()
            self.mm(ps[:, ba, :], self.onesb, self.sq, True, True, [("a_sq",), ("c",)], [("ps", ba)])
            self.act(self.rs, ps[:, ba, :], AF.Ln, [("ps", ba)], [("a_rs",)], bias=128.0 * RMS_EPS)
            self.act(self.rs, self.rs, AF.Exp, [("a_rs",)], [("a_rs",)], scale=-0.5)
            self.tt("dve", self.on, ps[:, 4 + hd, :], self.rs, ALU.mult, [("ps", 4 + hd), ("a_rs",)], [("a_on",)])
            self.stt("dve", oT[:, hd, :], self.on, self.ngs[:, j:j + 1], gsb[:, hd, :], ALU.mult, ALU.mult,
                     [("a_on",), ("a_gs", hd), ("c",)], [("a_oT", hd)])
        self.attention(j, i)
        import os
        dbg = os.environ.get("KDBG", "")
        if "zattn" in dbg:
            self.memset("dve", oT[:, 4:8, :], 0.0, [("a_oT", k) for k in range(4, 8)])
        if "zhgrn" in dbg:
            self.memset("dve", oT[:, 0:4, :], 0.0, [("a_oT", k) for k in range(4)])
        for kv in range(2):
            self.cp("pool", self.KTh[:, kv, :], KT[:, kv, 512:640], [("a_KT", kv, 4)], [("KTh",)])
        self.cp("pool", self.Vh, vtm[:, 3, 512:640], [("a_v", 3)], [("Vh",)])
        if "dumpo" in dbg:
            for k in range(KC):
                self.cp("dve", self.xT[:, k, :], oT[:, k, :], [("a_oT", k)], [("xT", k)])
            return
        for s in range(2):
            sl, wo = self.wload(f"e{j}_wo{s}")
            wo = wo.rearrange("p (a b) -> p a b", b=512)
            for c in range(4):
                m = 4 * s + c
                bo = self.bank()
                for k in range(KC):
                    self.mm(ps[:, bo, :], wo[:, k, c * 128:(c + 1) * 128], oT[:, k, :], k == 0, k == 7,
                            [("w", sl), ("a_oT", k)], [("ps", bo)])
                self.ln_accum(m, ps[:, bo, :], bo)
        self.ln_finish(self.lnp(0, layer), self.lnp(1, layer))

    def prepass(self, j, layer, i):
        sgm, KT, vtm = self.sgm, self.KT, self.vtm
        NT = self.NT
        self.load_x(layer, i)
        self.fence()
        self.proj_fm(f"e{j}_fb", 4, lambda c, p, b: self.act(sgm[:, 1, c, :], p, AF.Sigmoid, [("ps", b)], [("a_sgm", 1, c)]))
        self.proj_fm(f"e{j}_kd", 2, lambda c, p, b: self.cp("dve", KT[:, c, 128:256], p, [("ps", b)], [("a_KT", c, 1)]), ncols=128)
        self.proj_tm(j)
        for kv in range(2):
            self.cp("pool", self.hal[:, kv * 128:(kv + 1) * 128], KT[:, kv, 128:256], [("a_KT", kv, 1)], [("a_hal",)])
        self.cp("pool", self.hal[:, 256:384], vtm[:, 0, 512:640], [("a_v", 0)], [("a_hal",)])
        self.P.dma("act", self.halo[i], self.hal, [("a_hal",)], [("halo", i)], dsem="hals")
        of, _ = self.cindex["flags"]
        if i == NT - 1:
            self.memset("dve", self.S[:, 1, :], 0.0, [("S", 1)])
        elif (i + 1) % TPU == 0:
            u = (i + 1) // TPU
            self.act(self.S[:, 1, :], self.S[:, 1, :], AF.Identity, [("S", 1), ("c",)], [("S", 1)],
                     scale=self.cst[:, of + u:of + u + 1])
        self.P.dma("act", self.sbs[i], self.S[:, 1, :], [("S", 1)], [("sbs", i)], dsem="sbs")
        if i > 0:
            for hd in range(4):
                self.gates(j, 1, hd, need_q=False)
            self.ktrans(1)
            for c in range(7, -1, -1):
                self.state_step(1, c, False)
        self.fence()


def build_program(NU, windex, TOTW, cindex, NCST, npairs=2):
    from contextlib import ExitStack
    nc = bass.Bass("TRN2", target_bir_lowering=False)
    NT = NU * TPU
    NTOK = NU * UNIT
    xin = nc.dram_tensor("xin", [NTOK, D], F32, kind="ExternalInput").ap()
    wsrc = nc.dram_tensor("wsrc", [128, TOTW], F32, kind="ExternalInput").ap()
    cstd = nc.dram_tensor("cst", [128, NCST], F32, kind="ExternalInput").ap()
    cbsd = nc.dram_tensor("cbs", [128, 2048], F32, kind="ExternalInput").ap()
    yout = nc.dram_tensor("yout", [NTOK, D], F32, kind="ExternalOutput").ap()
    import os
    wbf = {name: nc.dram_tensor(f"wbf_{name}", [128, W], BF16, kind="Internal").ap() for name, (off, W) in windex.items()}
    xs = [nc.dram_tensor(f"xs{i}", [128, KC * T], F32, kind="Internal").ap() for i in range(NT)]
    sbs = nc.dram_tensor("sbs", [NT, 128, 512], F32, kind="Internal").ap()
    halo = nc.dram_tensor("halo", [NT, 128, 384], BF16, kind="Internal").ap()

    stack = ExitStack()
    with stack:
        NBYTES = 207 * 1024
        arena = stack.enter_context(nc.sbuf_tensor("arena", [128, NBYTES // 4], F32))
        psum = stack.enter_context(nc.psum_tensor("psum", [128, 8, 512], F32))
        P = Prog(nc)
        B = Builder(nc, P, NU, windex, cindex)
        B.xin, B.yout, B.wbf, B.xs, B.sbs, B.halo = xin, yout, wbf, xs, sbs, halo
        B.ps = psum
        B.spn = 0
        B.atn = 0
        M = Mem(arena, NBYTES)
        B.NSLOT = 3
        B.wslot = M.alloc([B.NSLOT, 4096], BF16)
        B.xT = M.alloc([KC, T], F32)
        B.xb = M.alloc([KC, T], BF16)
        B.rb = M.alloc([2, T], BF16)
        B.rsq = M.alloc([2, T], BF16)
        B.stat = M.alloc([4, T], F32)
        B.cst = M.alloc([NCST], F32)
        B.identf = B.cst[:, cindex["ident"][0]:cindex["ident"][0] + 128]
        B.scanmask = B.cst[:, cindex["scanmask"][0]:cindex["scanmask"][0] + 512]
        B.identb = M.alloc([128], BF16)
        B.onesb = M.alloc([128], BF16)
        B.DAB = M.alloc([2, 128], BF16)
        B.trif = M.alloc([512], BF16)
        B.trib = M.alloc([512], BF16)
        B.S = M.alloc([2, 512], F32)
        B.KTh = M.alloc([2, 128], BF16)
        B.Vh = M.alloc([128], BF16)
        B.wst = M.alloc([2, 8, 128], BF16)
        B.Cc = M.alloc([2, 16, 128], F32)
        B.lbt = M.alloc([2, 2, 4], F32)
        B.omlt = M.alloc([2, 2, 4], F32)
        B.nomlt = M.alloc([2, 2, 4], F32)
        B.esink = M.alloc([2, 4], F32)
        B.ngs = M.alloc([2], F32)
        B.fz = M.alloc([8], F32)
        base = M.mark()
        B.qs = M.alloc([4, T], F32)
        B.gsb = M.alloc([4, T], F32)
        B.sgm = M.alloc([2, 4, T], F32)
        B.gB = M.alloc([T], F32)
        B.gC = M.alloc([T], F32)
        B.gE = M.alloc([T], F32)
        B.qt = M.alloc([2, 4, T], BF16)
        B.kt = M.alloc([2, 4, T], BF16)
        B.kT = M.alloc([4, 512], BF16)
        B.vtm = M.alloc([4, 640], BF16)
        B.Sp = M.alloc([2, 512], BF16)
        B.tS = M.alloc([512], F32)
        B.AT = M.alloc([2, 512], BF16)
        B.oT = M.alloc([KC, T], BF16)
        B.QT = M.alloc([4, T], BF16)
        B.KT = M.alloc([2, 768], BF16)
        B.Vp = M.alloc([6, 2, 2, 128], BF16)
        B.PT = M.alloc([2, 6, 384], BF16)
        B.tsc = M.alloc([2, 384], F32)
        B.cs = M.alloc([2, 4, 4, 8], F32)
        B.sq = M.alloc([T], BF16)
        B.on = M.alloc([T], F32)
        B.rs = M.alloc([T], F32)
        B.hal = M.alloc([384], BF16)
        em_top = M.mark()
        M.reset(base)
        B.h = M.alloc([FC, T], BF16)
        B.sg = M.alloc([2, T], F32)
        M.reset(base)
        B.vtok = M.alloc([4, 2048], F32)
        B.vn = M.alloc([4, 2048], BF16)
        B.yin = M.alloc([16, T], BF16)
        B.uc = M.alloc([2, T], F32)
        B.tmp = M.alloc([2, T], F32)
        B.bst = M.alloc([4, 4, 6], F32)
        B.bmv = M.alloc([4, 2], F32)
        M.reset(base)
        B.xtok = M.alloc([4, D], F32)
        B.pc0 = M.alloc([4, D], BF16)
        B.pc1 = M.alloc([4, D], BF16)
        B.pc2 = M.alloc([4, D], BF16)
        M.reset(base)
        B.ytok = M.alloc([4, D], F32)
        M.reset(base)
        cbs_sb = M.alloc([2, 8, 128], F32)
        wtmp = M.alloc([1024], F32)

        dbg = os.environ.get("KDBG", "")
        names = list(windex.keys())
        for q, name in enumerate(names):
            a, W = windex[name]
            P.dma("pool", wbf[name], wsrc[:, a:a + W], [], [("wbfc", q)] + ([("wbf",)] if q == len(names) - 1 else []), dsem="cast")
        P.dma("sp", B.cst, cstd, [], [("c",)], dsem="cst")
        ci = cindex
        P.dma("sp", cbs_sb, cbsd.rearrange("p (l g t) -> p l g t", l=2, g=8), [], [("a_cbs",)], dsem="cst2")
        B.cp("dve", B.identb, B.identf, [("c",)], [("c",)])
        B.memset("dve", B.onesb, 1.0, [("c",)])
        B.memset("dve", B.DAB, 0.0, [("c",)])
        B.memset("dve", B.DAB[:, 0, 0:64], 1.0, [("c",)])
        B.memset("dve", B.DAB[:, 1, 64:128], 1.0, [("c",)])
        B.cp("dve", B.trif, B.cst[:, ci["trif"][0]:ci["trif"][0] + 512], [("c",)], [("c",)])
        B.cp("dve", B.trib, B.cst[:, ci["trib"][0]:ci["trib"][0] + 512], [("c",)], [("c",)])
        B.memset("dve", B.S, 0.0, [("S", 0), ("S", 1)])
        B.memset("dve", B.KTh, 0.0, [("KTh",)])
        B.memset("dve", B.Vh, 0.0, [("Vh",)])
        ol = ci["lbl"][0]
        lbl = B.cst[:, ol:ol + 16].rearrange("p (l d h) -> p l d h", l=2, d=2)
        B.memset("dve", B.lbt, 0.0, [("c",)])
        B.tt("dve", B.lbt[:, 1, :, :], lbl[:, 1, :, :], lbl[:, 0, :, :], ALU.subtract, [("c",)], [("c",)])
        B.act(B.lbt[:, 1, :, :], B.lbt[:, 1, :, :], AF.Sigmoid, [("c",)], [("c",)])
        B.ts("dve", B.omlt, B.lbt, -1.0, 1.0, ALU.mult, ALU.add, [("c",)], [("c",)])
        B.ts("dve", B.nomlt, B.omlt, -1.0, None, ALU.mult, None, [("c",)], [("c",)])
        osk = ci["sink"][0]
        B.act(B.esink, B.cst[:, osk:osk + 8].rearrange("p (l c) -> p l c", l=2), AF.Exp, [("c",)], [("c",)])
        ong = ci["ng"][0]
        B.ts("dve", B.ngs, B.cst[:, ong:ong + 2], float(np.sqrt(128.0)), None, ALU.mult, None, [("c",)], [("c",)])
        ob = ci["clnb"][0]
        for j in range(0 if "nowst" in dbg else 2):
            off, W = windex[f"o{j}_ws"]
            P.dma("sp", B.wst[:, j, :, :], wbf[f"o{j}_ws"].rearrange("p (g t) -> p g t", g=8), [("wbf",)], [("c",)], dsem=f"cst3{j}")
            for hf in range(2):
                B.mm(psum[:, hf, :], B.onesb, B.wst[:, j, 4 * hf:4 * hf + 4, :], True, True, [("c",)], [("ps", hf)])
            for fc in range(16):
                g = fc // 2
                B.stt("dve", B.Cc[:, j, fc, :], psum[:, g // 4, (g % 4) * 128:(g % 4 + 1) * 128],
                      B.cst[:, ob + j * 16 + fc:ob + j * 16 + fc + 1], cbs_sb[:, j, g, :], ALU.mult, ALU.add,
                      [("ps", g // 4), ("c",), ("a_cbs",)], [("c",)])
        B.fence()

        import os
        stages = os.environ.get("KSTAGES", "pre,em,ffn0,sgu,ffn1").split(",")
        for pair in range(npairs):
            le, lo = 2 * pair, 2 * pair + 1
            if "pre" in stages:
                for i in range(NT - 1, -1, -1):
                    B.prepass(pair, le, i)
            for i in range(NT):
                if "noloop" in dbg:
                    continue
                B.load_x(2 if "ldalt" in dbg else le, i)
                B.fence()
                if "nostore" in dbg:
                    continue
                if "em" in stages:
                    B.even_mixer(pair, le, i)
                    B.fence()
                if "ffn0" in stages:
                    B.ffn(le)
                    B.fence()
                if "sgu" in stages:
                    B.sgu(pair, lo)
                    B.fence()
                if "ffn1" in stages:
                    B.ffn(lo)
                    B.fence()
                B.store_x(3 if pair == npairs - 1 else 1, i)
                B.fence()
        P.emit(stack)
    return nc


_WKEYS = ("w_in_ab", "hgrn_lb_logits", "hgrn_norm_g", "attn_sink", "w_out_ab", "w_in_c", "c_ln_g", "c_ln_b", "c_ws",
          "c_bs", "w_out_c", "ffn_w_gate", "ffn_w_up", "ffn_w_down", "ln_mix_g", "ln_mix_b", "ln_ffn_g", "ln_ffn_b")


def run_cores(xs_per_core, flags_per_core, weights, npairs=2, trace=False):
    NU = flags_per_core[0].shape[0]
    w = {k: np.asarray(weights[k], dtype=np.float32) for k in _WKEYS}
    slabs, windex = build_slabs(w)
    wsrc = np.ascontiguousarray(np.concatenate(slabs, axis=1))
    TOTW = wsrc.shape[1]
    in_maps = []
    for x, fl in zip(xs_per_core, flags_per_core):
        cst, cindex, cbs = build_consts(w, np.asarray(fl, np.float32))
        in_maps.append({"xin": np.ascontiguousarray(x, dtype=np.float32), "wsrc": wsrc, "cst": cst, "cbs": cbs})
    nc = build_program(NU, windex, TOTW, cindex, cst.shape[1], npairs=npairs)
    res = run_bass_kernel_spmd(nc, in_maps, core_ids=list(range(len(in_maps))), trace=trace)
    return [r["yout"] for r in res.results], res


def kernel(x_prompt, x_sample, **weights):
    x_prompt = np.asarray(x_prompt, dtype=np.float32)
    x_sample = np.asarray(x_sample, dtype=np.float32)
    NU = 8
    nseq = x_prompt.shape[0]
    assign = [[] for _ in range(8)]
    for s in range(nseq):
        assign[1 + s % 7].append(s)
    xs, fls = [x_sample[0]], [np.array([0.0] + [1.0] * (NU - 1), np.float32)]
    for c in range(1, 8):
        ids = assign[c] + [assign[c][-1]] * (NU - len(assign[c]))
        xs.append(x_prompt[ids].reshape(NU * UNIT, D))
        fls.append(np.zeros(NU, np.float32))
    outs, _ = run_cores(xs, fls, weights)
    y_sample = outs[0].reshape(1, NU * UNIT, D)
    y_prompt = np.empty_like(x_prompt)
    for c in range(1, 8):
        o = outs[c].reshape(NU, UNIT, D)
        for n, s in enumerate(assign[c]):
            y_prompt[s] = o[n]
    return (y_prompt, y_sample)
```
